# Optimizing a Trainium2 kernel written in Bass

```python
import jax, jax.numpy as jnp
from jax import lax
import numpy as np

D_MODEL = 1024
BATCH = 4
SEQ = 8192
DEPTH = 1

CHUNK = 64
LEFT_CHUNKS = 8
BAND = (LEFT_CHUNKS + 1) * CHUNK
N_HEADS = 8
HEAD_DIM = 64
D_ATTN = N_HEADS * HEAD_DIM
REL_CLIP = 128
D_CONV = 512
CONV_WIDTH = 31
N_BRANCH = 2
D_IN = 3 * D_ATTN + 2 * D_CONV + N_BRANCH * D_MODEL
D_FF = 2816
EPS = 1e-6

kernel_name = "hybrid_chunked_attn_conformer_conv_macaron"


def rms_norm(x, g):
    xf = x.astype(jnp.float32)
    y = xf * lax.rsqrt(jnp.mean(xf * xf, axis=-1, keepdims=True) + EPS)
    return (y * g.astype(jnp.float32)).astype(x.dtype)


def layer_norm(x, g, b):
    xf = x.astype(jnp.float32)
    mu = jnp.mean(xf, axis=-1, keepdims=True)
    var = jnp.mean(jnp.square(xf - mu), axis=-1, keepdims=True)
    y = (xf - mu) * lax.rsqrt(var + EPS)
    return (y * g.astype(jnp.float32) + b.astype(jnp.float32)).astype(x.dtype)


def swiglu(x, w_gate, w_up, w_down):
    return (jax.nn.silu(x @ w_gate) * (x @ w_up)) @ w_down


def chunked_attention(q, k, v, rel_table):
    B, S, H, Dh = q.shape
    nc = S // CHUNK
    qc = q.reshape(B, nc, CHUNK, H, Dh)
    pad = ((0, 0), (LEFT_CHUNKS, 0), (0, 0), (0, 0), (0, 0))
    kp = jnp.pad(k.reshape(B, nc, CHUNK, H, Dh), pad)
    vp = jnp.pad(v.reshape(B, nc, CHUNK, H, Dh), pad)
    k_band = jnp.concatenate([kp[:, j:j + nc] for j in range(LEFT_CHUNKS + 1)], axis=2)
    v_band = jnp.concatenate([vp[:, j:j + nc] for j in range(LEFT_CHUNKS + 1)], axis=2)
    scores = jnp.einsum('bnqhd,bnkhd->bnhqk', qc, k_band).astype(jnp.float32) * (HEAD_DIM ** -0.5)
    qi = jnp.arange(CHUNK)[:, None]
    kj = jnp.arange(BAND)[None, :]
    dist = qi + LEFT_CHUNKS * CHUNK - kj
    idx = jnp.clip(dist, -REL_CLIP, REL_CLIP) + REL_CLIP
    bias = rel_table.astype(jnp.float32)[:, idx]
    scores = scores + bias[None, None]
    key_chunk = jnp.arange(nc)[:, None] + (jnp.arange(BAND) // CHUNK)[None, :] - LEFT_CHUNKS
    valid = (key_chunk >= 0)[None, :, None, None, :]
    scores = jnp.where(valid, scores, jnp.float32(-1e30))
    p = jax.nn.softmax(scores, axis=-1).astype(v.dtype)
    out = jnp.einsum('bnhqk,bnkhd->bnqhd', p, v_band)
    return out.reshape(B, S, H * Dh)


def conv_module(c_in, glu_bias, dw_w, dw_b, ln_g, ln_b, w_out):
    c = c_in + glu_bias
    c = c[..., :D_CONV] * jax.nn.sigmoid(c[..., D_CONV:])
    c = lax.conv_general_dilated(
        c, dw_w.astype(c.dtype), window_strides=(1,), padding=((CONV_WIDTH - 1, 0),),
        dimension_numbers=('NWC', 'WIO', 'NWC'), feature_group_count=D_CONV) + dw_b
    c = jax.nn.silu(layer_norm(c, ln_g, ln_b))
    return c @ w_out


def hybrid_mixer(u, w_in, gate_bias, rel_table, w_attn_out, conv_glu_bias, conv_dw_w,
                 conv_dw_b, conv_ln_g, conv_ln_b, conv_w_out, w_out):
    B, S, _ = u.shape
    proj = u @ w_in
    q, k, v, c_in, g = jnp.split(
        proj, [D_ATTN, 2 * D_ATTN, 3 * D_ATTN, 3 * D_ATTN + 2 * D_CONV], axis=-1)
    shp = (B, S, N_HEADS, HEAD_DIM)
    y_a = chunked_attention(q.reshape(shp), k.reshape(shp), v.reshape(shp), rel_table) @ w_attn_out
    y_b = conv_module(c_in, conv_glu_bias, conv_dw_w, conv_dw_b, conv_ln_g, conv_ln_b, conv_w_out)
    gates = jax.nn.sigmoid(g + gate_bias)
    merged = gates[..., :D_MODEL] * y_a + gates[..., D_MODEL:] * y_b
    return merged @ w_out


def setup_inputs(seed: int = 0) -> dict:
    key = jax.random.key(seed)
    ks = jax.random.split(key, 32)
    L, D = DEPTH, D_MODEL

    def w(k, shape, fan_in):
        return jax.random.normal(k, shape, jnp.float32) * (fan_in ** -0.5)

    def gain(k, shape):
        return 1.0 + 0.05 * jax.random.normal(k, shape, jnp.float32)

    def small(k, shape, s=0.02):
        return s * jax.random.normal(k, shape, jnp.float32)

    return {
        "x": jax.random.normal(ks[0], (BATCH, SEQ, D), jnp.float32),
        "ffn1_norm_pre": gain(ks[1], (L, D)),
        "ffn1_w_gate": w(ks[2], (L, D, D_FF), D),
        "ffn1_w_up": w(ks[3], (L, D, D_FF), D),
        "ffn1_w_down": w(ks[4], (L, D_FF, D), D_FF),
        "ffn1_norm_post": gain(ks[5], (L, D)),
        "mix_norm_pre": gain(ks[6], (L, D)),
        "w_in": w(ks[7], (L, D, D_IN), D),
        "gate_bias": small(ks[8], (L, N_BRANCH * D), 0.1),
        "rel_table": small(ks[9], (L, N_HEADS, 2 * REL_CLIP + 1), 0.5),
        "w_attn_out": w(ks[10], (L, D_ATTN, D), D_ATTN),
        "conv_glu_bias": small(ks[11], (L, 2 * D_CONV)),
        "conv_dw_w": w(ks[12], (L, CONV_WIDTH, 1, D_CONV), CONV_WIDTH),
        "conv_dw_b": small(ks[13], (L, D_CONV)),
        "conv_ln_g": gain(ks[14], (L, D_CONV)),
        "conv_ln_b": small(ks[15], (L, D_CONV)),
        "conv_w_out": w(ks[16], (L, D_CONV, D), D_CONV),
        "w_out": w(ks[17], (L, D, D), D),
        "mix_norm_post": gain(ks[18], (L, D)),
        "ffn2_norm_pre": gain(ks[19], (L, D)),
        "ffn2_w_gate": w(ks[20], (L, D, D_FF), D),
        "ffn2_w_up": w(ks[21], (L, D, D_FF), D),
        "ffn2_w_down": w(ks[22], (L, D_FF, D), D_FF),
        "ffn2_norm_post": gain(ks[23], (L, D)),
    }


def reference(x, ffn1_norm_pre, ffn1_w_gate, ffn1_w_up, ffn1_w_down, ffn1_norm_post,
              mix_norm_pre, w_in, gate_bias, rel_table, w_attn_out, conv_glu_bias,
              conv_dw_w, conv_dw_b, conv_ln_g, conv_ln_b, conv_w_out, w_out, mix_norm_post,
              ffn2_norm_pre, ffn2_w_gate, ffn2_w_up, ffn2_w_down, ffn2_norm_post):
    h = x
    for l in range(DEPTH):
        f = swiglu(rms_norm(h, ffn1_norm_pre[l]), ffn1_w_gate[l], ffn1_w_up[l], ffn1_w_down[l])
        h = h + 0.5 * rms_norm(f, ffn1_norm_post[l])
        m = hybrid_mixer(rms_norm(h, mix_norm_pre[l]), w_in[l], gate_bias[l], rel_table[l],
                         w_attn_out[l], conv_glu_bias[l], conv_dw_w[l], conv_dw_b[l],
                         conv_ln_g[l], conv_ln_b[l], conv_w_out[l], w_out[l])
        h = h + rms_norm(m, mix_norm_post[l])
        f = swiglu(rms_norm(h, ffn2_norm_pre[l]), ffn2_w_gate[l], ffn2_w_up[l], ffn2_w_down[l])
        h = h + 0.5 * rms_norm(f, ffn2_norm_post[l])
    return h
```

```python
import numpy as np
from contextlib import ExitStack
import concourse.bass as bass
import concourse.mybir as mybir
from concourse.bass_utils import run_bass_kernel_spmd

F32 = mybir.dt.float32
BF16 = mybir.dt.bfloat16
ALU = mybir.AluOpType
AF = mybir.ActivationFunctionType
AX = mybir.AxisListType

ENG = ("pe", "act", "dve", "pool", "sp")
EPS = 1e-6
D = 1024
DFF = 2816
NCH = 22
NCORES = 8
TOK = 4096
HALO = 512

C_GPRE1, C_GPREM, C_GPRE2 = 0, 8, 16
C_GATEB = 24
C_GLUA, C_GLUB = 40, 44
C_DWW = 48
C_DWB = 48 + 124
C_LNG = C_DWB + 4
C_LNB = C_LNG + 4
NCST = C_LNB + 4
MAX_OPS = None
CONV_RATE = 3
EVAC_ACT = (0,)


class Op:
    __slots__ = ("eng", "fn", "dma", "pos", "flag", "waits", "dma_target", "done_clock")


class Prog:
    def __init__(self):
        self.ops = {e: [] for e in ENG}
        self.comp = {e: [] for e in ENG}
        self.last_w = {}
        self.readers = {}
        self.dma_count = {}
        self.eng_clock = {e: {} for e in ENG}
        self.phase_key = "phase"

    def add(self, eng, fn, reads=(), writes=(), dma=None):
        self.n_added = getattr(self, "n_added", 0) + 1
        if MAX_OPS is not None and self.n_added > MAX_OPS:
            return None
        op = Op()
        op.eng = eng
        op.fn = fn
        op.dma = dma
        op.flag = False
        reads = list(reads)
        reads.append(self.phase_key)
        deps = []
        lw = self.last_w
        rd = self.readers
        for k in reads:
            d = lw.get(k)
            if d is not None:
                deps.append(d)
            if isinstance(k, tuple) and (k[0] == "pf" or k[0] == "pb"):
                r = rd.get(k)
                if r:
                    deps.extend(x for x in r if x.eng != eng)
        for k in writes:
            d = lw.get(k)
            if d is not None:
                deps.append(d)
            r = rd.get(k)
            if r:
                deps.extend(r)
        clock = self.eng_clock[eng]
        waits = {}
        pe_self = (eng == "pe" and dma is None)
        for d in deps:
            if d.dma is not None:
                dim = ("dma", d.dma)
                val = d.dma_target
            else:
                if pe_self and d.eng == "pe":
                    continue
                dim = d.eng
                val = d.pos
            if clock.get(dim, 0) >= val:
                continue
            if waits.get(dim, 0) < val:
                waits[dim] = val
        for d in deps:
            if d.dma is None and pe_self and d.eng == "pe":
                continue
            for dim, val in d.done_clock.items():
                if clock.get(dim, 0) < val:
                    clock[dim] = val
        for dim, val in waits.items():
            if clock.get(dim, 0) < val:
                clock[dim] = val
            if not isinstance(dim, tuple):
                self.comp[dim][val - 1].flag = True
        op.waits = waits
        if fn is None:
            op.pos = None
            op.done_clock = None
            self.ops[eng].append(op)
            return op
        dc = dict(clock)
        if dma is not None:
            c = self.dma_count.get(dma, 0) + 1
            self.dma_count[dma] = c
            op.dma_target = c
            op.pos = None
            dc[("dma", dma)] = c
        else:
            self.comp[eng].append(op)
            op.pos = len(self.comp[eng])
            dc[eng] = op.pos
        op.done_clock = dc
        self.ops[eng].append(op)
        for k in reads:
            rd.setdefault(k, []).append(op)
        for k in writes:
            lw[k] = op
            rd[k] = []
        return op

    def emit(self, block, eng_sems, dma_sems):
        counts = {}
        for e in ENG:
            c = 0
            arr = []
            for op in self.comp[e]:
                if op.flag:
                    c += 1
                arr.append(c)
            counts[e] = arr

        def run(e, engine):
            for op in self.ops[e]:
                for dim, val in op.waits.items():
                    if isinstance(dim, tuple):
                        engine.wait_ge(dma_sems[dim[1]], 16 * val)
                    else:
                        engine.wait_ge(eng_sems[dim], counts[dim][val - 1])
                if op.fn is None:
                    continue
                ins = op.fn(engine)
                if op.dma is not None:
                    ins.then_inc(dma_sems[op.dma], 16)
                elif op.flag:
                    ins.then_inc(eng_sems[e], 1)

        block.tensor(lambda eng: run("pe", eng))
        block.scalar(lambda eng: run("act", eng))
        block.vector(lambda eng: run("dve", eng))
        block.gpsimd(lambda eng: run("pool", eng))
        block.sync(lambda eng: run("sp", eng))


class Rot:
    def __init__(self, n):
        self.n = n
        self.i = 0

    def next(self):
        v = self.i
        self.i = (self.i + 1) % self.n
        return v


def build_nc(ng=4):
    nc = bass.Bass("TRN2", target_bir_lowering=False)

    def din(name, shape):
        return nc.dram_tensor(name, shape, F32, kind="ExternalInput").ap()

    xo = din("xo", [TOK, D])
    xh = din("xh", [HALO, D])
    hm_d = din("hm", [128, 2])
    wgu_d = [din("wgu1", [NCH, 128, 2048]), din("wgu2", [NCH, 128, 2048])]
    wd_d = [din("wd1", [NCH, 128, 1024]), din("wd2", [NCH, 128, 1024])]
    winb_d = din("winb", [36, 128, 1024])
    wv_d = din("wv", [128, 4096])
    wao_d = din("wao", [128, 4096])
    wco_d = din("wco", [128, 4096])
    wout_d = din("wout", [128, 8192])
    bias_d = din("biasT", [128, 5120])
    cst_d = din("cst", [128, NCST])
    gpost_d = din("gpost", [3, 128, 1024])
    ident_d = din("ident", [128, 128])
    y_d = nc.dram_tensor("y", [TOK, D], F32, kind="ExternalOutput").ap()

    P = Prog()
    global LASTP
    LASTP = P
    es = ExitStack()
    with es:
        def sb(name, shape, dt):
            return es.enter_context(nc.sbuf_tensor(name, shape, dt))

        h = sb("h", [128, 8, 1024], F32)
        kT = sb("kT", [128, 4, 1024], BF16)
        vt = sb("vt", [128, 8, 512], BF16)
        gp = sb("gp", [128, 1024], F32)
        junk = sb("junk", [128, 2, 1024], BF16)
        mhalf = sb("mhalf", [128, 512], F32)
        st = sb("st", [128, 64], F32)
        ast = sb("ast", [128, 16], F32)
        rsum = sb("rsum", [128, 2, 8], F32)
        rinv = sb("rinv", [128, 2, 8], F32)
        cst = sb("cst_sb", [128, NCST], F32)
        identf = sb("identf", [128, 128], F32)
        identb = sb("identb", [128, 128], BF16)
        ones32 = sb("ones32", [128, 128], F32)
        hm = sb("hm_sb", [128, 2], F32)
        bscr = sb("bscr", [128, 2], F32)
        cgh_p = sb("cgh_p", [128, 4, 30], F32)
        ARENA_WORDS = 36800
        arena = sb("arena", [128, ARENA_WORDS], F32)
        pf = es.enter_context(nc.psum_tensor("pf", [128, 3072], F32))
        pb = es.enter_context(nc.psum_tensor("pb", [128, 2048], BF16))

        class Arena:
            def __init__(self):
                self.off = 0

            def take(self, free_shape, dt):
                nel = int(np.prod(free_shape))
                words = nel if dt == F32 else (nel + 1) // 2
                assert self.off + words <= ARENA_WORDS, (self.off, words)
                ap = arena[:, self.off:self.off + words]
                if dt == BF16:
                    ap = ap.bitcast(BF16)
                self.off += words
                if len(free_shape) == 2:
                    ap = ap.rearrange("p (a b) -> p a b", a=free_shape[0])
                elif len(free_shape) == 3:
                    ap = ap.rearrange("p (a b c) -> p a b c", a=free_shape[0], b=free_shape[1])
                return ap

        A = Arena()
        xnT = A.take((8, 1024), BF16)
        hT = A.take((NCH, 1024), BF16)
        wdt = A.take((NCH, 1024), BF16)
        NSG = 4
        wgus = [A.take((2, 8, 128), BF16) for _ in range(NSG)]
        sg = [A.take((512,), F32) for _ in range(2)]
        xn_f = [A.take((1024,), BF16) for _ in range(2)]
        ptmp_f = [A.take((512,), F32) for _ in range(2)]
        M = Arena()
        uT = M.take((8, 512), BF16)
        qT = M.take((4, 512), BF16)
        biasT = M.take((8, 640), F32)
        cg = M.take((4, 542), F32)
        acc = M.take((4, 512), F32)
        sq = M.take((512,), F32)
        mean = M.take((512,), F32)
        tmpv = M.take((512,), F32)
        rstdt = M.take((512,), F32)
        so = M.take((4, 512), BF16)
        attnT = M.take((4, 512), BF16)
        mT = M.take((8, 512), BF16)
        s_sb = [M.take((640,), F32) for _ in range(2)]
        p_sb = [M.take((640,), BF16) for _ in range(2)]
        PT_sb = [M.take((640,), BF16) for _ in range(2)]
        o_sb = M.take((512,), BF16)
        sA = M.take((512,), F32)
        sB = M.take((512,), F32)
        NSW = 4
        wsl = [M.take((8, 128), BF16) for _ in range(NSW)]
        wvt = M.take((8, 512), BF16)
        waot = M.take((4, 1024), BF16)
        wcot = M.take((4, 1024), BF16)
        woutt = M.take((8, 1024), BF16)
        xn_m = [M.take((1024,), BF16) for _ in range(2)]
        ptmp_m = [M.take((512,), F32) for _ in range(2)]

        sems = {e: es.enter_context(nc.semaphore("s_" + e)) for e in ENG}
        dkeys = (["h%d" % i for i in range(8)] + ["o%d" % i for i in range(8)] +
                 ["wgu%d" % i for i in range(NSG)] + ["wd", "gp", "cst", "id", "hm", "bias",
                                                      "wv", "wao", "wco", "wout"] +
                 ["ws%d" % i for i in range(NSW)] + ["rollk", "rollv"])
        dsems = {k: es.enter_context(nc.semaphore("d_" + k)) for k in dkeys}
        block = es.enter_context(nc.Block())

        def pfb(b, n=512, o=0):
            return pf[:, b * 512 + o:b * 512 + o + n]

        def pbb(b, n, o=0):
            return pb[:, b * 1024 + o:b * 1024 + o + n]

        r_st = Rot(16)
        r_junk = Rot(2)
        r_pb = Rot(2)
        r_xn = Rot(2)
        r_pt = Rot(2)
        r_g = Rot(2)
        r_u = Rot(2)
        r_sg = Rot(2)
        r_f = Rot(3)
        r_proj = Rot(4)
        r_s = Rot(2)
        r_ssb = Rot(2)
        r_p = Rot(2)
        r_PT = Rot(2)
        r_ast = Rot(16)
        r_o = Rot(2)
        r_alt = Rot(2)
        barrier_n = [0]

        def barrier():
            barrier_n[0] += 1
            old = P.phase_key
            op_fn = lambda e: e.memset(bscr[:, 0:1], 0.0)
            P.phase_key = "nophase"
            P.add("dve", op_fn, reads=[], writes=[old])
            P.phase_key = old

        P.add("sp", lambda e: e.dma_start(out=cst[:], in_=cst_d), writes=["cst"], dma="cst")
        P.add("sp", lambda e: e.dma_start(out=identf[:], in_=ident_d), writes=["identf"], dma="id")
        P.add("sp", lambda e: e.dma_start(out=hm[:], in_=hm_d), writes=["hm"], dma="hm")
        P.add("dve", lambda e: e.tensor_copy(out=identb[:], in_=identf[:]), reads=["identf"], writes=["identb"])
        P.add("pool", lambda e: e.memset(mhalf[:], -0.5), writes=["mhalf"])
        P.add("pool", lambda e: e.memset(ones32[:], 1.0), writes=["ones32"])

        def rstd_from(srcs, reads):
            s = r_st.next()
            key = ("st", s)
            n = srcs[0].shape[-1]
            for idx, src in enumerate(srcs):
                jk = r_junk.next()
                P.add("act", lambda e, src=src, c=4 * s + idx, jk=jk: e.activation(
                    out=junk[:, jk, 0:n], in_=src, func=AF.Square, accum_out=st[:, c:c + 1]),
                    reads=reads, writes=[(key, idx), ("junk", jk)])
            if len(srcs) == 2:
                P.add("dve", lambda e, s=s: e.tensor_tensor(
                    out=st[:, 4 * s:4 * s + 1], in0=st[:, 4 * s:4 * s + 1], in1=st[:, 4 * s + 1:4 * s + 2],
                    op=ALU.add),
                    reads=[(key, 0), (key, 1)], writes=[(key, 0)])
            P.add("dve", lambda e, s=s: e.tensor_scalar(
                out=st[:, 4 * s + 2:4 * s + 3], in0=st[:, 4 * s:4 * s + 1], scalar1=1.0 / 1024,
                scalar2=EPS, op0=ALU.mult, op1=ALU.add),
                reads=[(key, 0), (key, 1)], writes=[(key, 2)])
            P.add("pool", lambda e, s=s: e.tensor_tensor(
                out=st[:, 4 * s + 3:4 * s + 4], in0=st[:, 4 * s + 2:4 * s + 3], in1=mhalf[:, 0:1], op=ALU.pow),
                reads=[(key, 2), "mhalf"], writes=[(key, 3)])
            return st[:, 4 * s + 3:4 * s + 4], (key, 3)

        def prenorm_stats(hi):
            return rstd_from([h[:, hi, :]], [("h", hi)])

        def prenorm_apply(hi, rr, xn_bufs, dst, dstkey, col0, g0):
            r, rkey = rr
            xs = r_xn.next()
            xn = xn_bufs[xs]
            P.add("dve", lambda e: e.tensor_scalar(out=xn, in0=h[:, hi, :], scalar1=r, scalar2=None,
                                                   op0=ALU.mult),
                  reads=[("h", hi), rkey], writes=[("xn", xs)])
            tb = r_pb.next()
            for c in range(8):
                P.add("pe", lambda e, c=c: e.transpose(out=pbb(tb, 128, c * 128), in_=xn[:, c * 128:(c + 1) * 128],
                                                       identity=identb[:]),
                      reads=[("xn", xs), "identb"], writes=[("pb", tb)])
            for c in range(8):
                o = dst[:, c, col0:col0 + 128]
                i_ = pbb(tb, 128, c * 128)
                g = cst[:, g0 + c:g0 + c + 1]
                if tb == 0:
                    P.add("act", lambda e, o=o, i_=i_, g=g: e.activation(out=o, in_=i_, func=AF.Copy, scale=g),
                          reads=[("pb", tb), "cst"], writes=[(dstkey, c)])
                else:
                    P.add("dve", lambda e, o=o, i_=i_, g=g: e.tensor_scalar(out=o, in0=i_, scalar1=g, scalar2=None,
                                                                         op0=ALU.mult),
                          reads=[("pb", tb), "cst"], writes=[(dstkey, c)])

        def prenorm_tile(hi, xn_bufs, dst, dstkey, col0, g0):
            prenorm_apply(hi, prenorm_stats(hi), xn_bufs, dst, dstkey, col0, g0)

        def postnorm(hi, banks, factor, ptmp):
            r, rkey = rstd_from([pfb(banks[0]), pfb(banks[1])], [("pf", banks[0]), ("pf", banks[1])])
            for half in range(2):
                k = r_pt.next()
                tmp = ptmp[k]
                b = banks[half]
                P.add("dve", lambda e, tmp=tmp, b=b, half=half: e.scalar_tensor_tensor(
                    out=tmp, in0=pfb(b), scalar=r, in1=gp[:, half * 512:(half + 1) * 512],
                    op0=ALU.mult, op1=ALU.mult),
                    reads=[("pf", b), rkey, "gp"], writes=[("ptmp", k)])
                P.add("dve", lambda e, tmp=tmp, half=half: e.scalar_tensor_tensor(
                    out=h[:, hi, half * 512:(half + 1) * 512], in0=tmp, scalar=float(factor),
                    in1=h[:, hi, half * 512:(half + 1) * 512], op0=ALU.mult, op1=ALU.add),
                    reads=[("ptmp", k), ("h", hi)], writes=[("h", hi)])

        def ffn(which, htiles, xsrc, xoff, outdst, ooff, pre_done=False, fuse_next=None):
            nt = len(htiles)
            nsub = nt // 4
            g0 = C_GPRE1 if which == 0 else C_GPRE2
            if not pre_done:
                barrier()
            P.add("sp", lambda e: e.dma_start(out=gp[:], in_=gpost_d[0 if which == 0 else 2]),
                  writes=["gp"], dma="gp")

            def load_wgu(w_, c):
                slot = c % NSG
                P.add("pool", lambda e: e.dma_start(
                    out=wgus[slot].rearrange("p a b c -> p (a b c)"), in_=wgu_d[w_][c]),
                    writes=[("wgu", slot)], dma="wgu%d" % slot)

            def load_wd(c):
                P.add("pool", lambda e: e.dma_start(out=wdt[:, c, :], in_=wd_d[which][c]),
                      writes=[("wd", c)], dma="wd")

            if not pre_done:
                rrs = []
                for ti, hi in enumerate(htiles):
                    if xsrc is not None:
                        P.add("sp", lambda e, ti=ti, hi=hi: e.dma_start(
                            out=h[:, hi, :], in_=xsrc[xoff + ti * 128:xoff + (ti + 1) * 128, :]),
                            writes=[("h", hi)], dma="h%d" % hi)
                for ti, hi in enumerate(htiles):
                    rrs.append(prenorm_stats(hi))
                for c in range(NSG):
                    load_wgu(which, c)
                for ti, hi in enumerate(htiles):
                    prenorm_apply(hi, rrs[ti], xn_f, xnT, ("xnT", ti), ti * 128, g0)
            for c in range(NSG):
                load_wd(c)
            for c in range(NCH):
                slot = c % NSG
                for sub in range(nsub):
                    gb = r_g.next()
                    ub = 2 + r_u.next()
                    for kc in range(8):
                        P.add("pe", lambda e, kc=kc, gb=gb, sub=sub, slot=slot: e.matmul(
                            pfb(gb), lhsT=wgus[slot][:, 0, kc, :], rhs=xnT[:, kc, sub * 512:(sub + 1) * 512],
                            start=(kc == 0), stop=(kc == 7)),
                            reads=[("wgu", slot)] + [(("xnT", t), kc) for t in range(4 * sub, 4 * sub + 4)],
                            writes=[("pf", gb)])
                    for kc in range(8):
                        P.add("pe", lambda e, kc=kc, ub=ub, sub=sub, slot=slot: e.matmul(
                            pfb(ub), lhsT=wgus[slot][:, 1, kc, :], rhs=xnT[:, kc, sub * 512:(sub + 1) * 512],
                            start=(kc == 0), stop=(kc == 7)),
                            reads=[("wgu", slot)] + [(("xnT", t), kc) for t in range(4 * sub, 4 * sub + 4)],
                            writes=[("pf", ub)])
                    sgs = r_sg.next()
                    P.add("act", lambda e, gb=gb, sgs=sgs: e.activation(out=sg[sgs], in_=pfb(gb), func=AF.Silu),
                          reads=[("pf", gb)], writes=[("sg", sgs)])
                    P.add("dve", lambda e, ub=ub, sgs=sgs, sub=sub, c=c: e.tensor_tensor(
                        out=hT[:, c, sub * 512:(sub + 1) * 512], in0=sg[sgs], in1=pfb(ub), op=ALU.mult),
                        reads=[("sg", sgs), ("pf", ub)], writes=[("hT", c, sub)])
                if c + NSG < NCH:
                    load_wgu(which, c + NSG)
                    load_wd(c + NSG)
            if fuse_next is not None:
                for c in range(NSG):
                    load_wgu(0, c)
            P.add("pe", None, reads=[("wd", c) for c in range(NCH)])
            nrr = {}
            for ti, hi in enumerate(htiles):
                sub = ti // 4
                fb = r_f.next()
                banks = (2 * fb, 2 * fb + 1)
                for half in range(2):
                    for c in range(NCH):
                        P.add("pe", lambda e, c=c, half=half, ti=ti, banks=banks: e.matmul(
                            pfb(banks[half]), lhsT=hT[:, c, ti * 128:(ti + 1) * 128],
                            rhs=wdt[:, c, half * 512:(half + 1) * 512], start=(c == 0), stop=(c == NCH - 1)),
                            reads=[("hT", c, sub), ("wd", c)], writes=[("pf", banks[half])])
                if fuse_next is not None and ti >= 2:
                    t2 = ti - 2
                    prenorm_apply(htiles[t2], nrr[t2], xn_f, xnT, ("xnT", t2), t2 * 128, C_GPRE1)
                postnorm(hi, banks, 0.5, ptmp_f)
                if outdst is not None:
                    P.add("sp", lambda e, ti=ti, hi=hi: e.dma_start(
                        out=outdst[ooff + ti * 128:ooff + (ti + 1) * 128, :], in_=h[:, hi, :]),
                        reads=[("h", hi)], writes=[("y", hi)], dma="o%d" % hi)
                if fuse_next is not None:
                    nx, nxoff = fuse_next
                    P.add("sp", lambda e, ti=ti, hi=hi, nx=nx, nxoff=nxoff: e.dma_start(
                        out=h[:, hi, :], in_=nx[nxoff + ti * 128:nxoff + (ti + 1) * 128, :]),
                        writes=[("h", hi)], dma="h%d" % hi)
                    nrr[ti] = prenorm_stats(hi)
            if fuse_next is not None:
                for t2 in range(nt - 2, nt):
                    prenorm_apply(htiles[t2], nrr[t2], xn_f, xnT, ("xnT", t2), t2 * 128, C_GPRE1)

        class Stream:
            def __init__(self, blocks):
                self.blocks = blocks
                self.issued = 0
                self.cons = 0
                for _ in range(min(NSW, len(blocks))):
                    self._issue()

            def _issue(self):
                n = self.issued
                slot = n % NSW
                b = self.blocks[n]
                P.add("pool", lambda e: e.dma_start(out=wsl[slot].rearrange("p a b -> p (a b)"), in_=winb_d[b]),
                      writes=[("ws", slot)], dma="ws%d" % slot)
                self.issued += 1

            def get(self):
                slot = self.cons % NSW
                self.cons += 1
                return slot

            def release(self, count=1):
                for _ in range(count):
                    if self.issued < len(self.blocks):
                        self._issue()

        def mixer_load_weights(partial):
            P.add("pool", lambda e: e.dma_start(out=wvt.rearrange("p a b -> p (a b)"), in_=wv_d),
                  writes=["wv"], dma="wv")

        def mixer_load_late():
            P.add("sp", lambda e: e.dma_start(out=biasT.rearrange("p a b -> p (a b)"), in_=bias_d),
                  writes=["biasT"], dma="bias")
            P.add("sp", lambda e: e.dma_start(out=gp[:], in_=gpost_d[1]), writes=["gp"], dma="gp")
            P.add("pool", lambda e: e.dma_start(out=waot.rearrange("p a b -> p (a b)"), in_=wao_d),
                  writes=["wao"], dma="wao")
            P.add("pool", lambda e: e.dma_start(out=wcot.rearrange("p a b -> p (a b)"), in_=wco_d),
                  writes=["wco"], dma="wco")
            P.add("pool", lambda e: e.dma_start(out=woutt.rearrange("p a b -> p (a b)"), in_=wout_d),
                  writes=["wout"], dma="wout")

        def mixer(htiles, partial, halo_mask, late_loads=False):
            blocks = [0, 1, 2, 3, 4, 5, 6, 7, 12, 16, 13, 17, 14, 18, 15, 19]
            if not partial:
                for dc in range(8):
                    blocks += [20 + dc, 28 + dc]
            rrs = [prenorm_stats(hi) for hi in htiles]
            stream = Stream(blocks)
            if late_loads or partial:
                mixer_load_weights(partial)
            if late_loads:
                mixer_load_late()
            for ti, hi in enumerate(htiles):
                prenorm_apply(hi, rrs[ti], xn_m, uT, ("uT", ti), ti * 128, C_GPREM)
            uT_reads = [(("uT", t), kc) for t in range(4) for kc in range(8)]

            def proj(slot, bank):
                for kc in range(8):
                    P.add("pe", lambda e, kc=kc: e.matmul(pfb(bank), lhsT=wsl[slot][:, kc, :], rhs=uT[:, kc, :],
                                                          start=(kc == 0), stop=(kc == 7)),
                          reads=[("ws", slot)] + [(("uT", t), kc) for t in range(4)], writes=[("pf", bank)])

            for j in range(4):
                slot = stream.get()
                bank = r_proj.next()
                proj(slot, bank)
                P.add("act", lambda e, j=j, bank=bank: e.activation(out=qT[:, j, :], in_=pfb(bank), func=AF.Copy,
                                                                    scale=0.125),
                      reads=[("pf", bank)], writes=[("qT", j)])
                stream.release()
            for j in range(4):
                slot = stream.get()
                bank = r_proj.next()
                proj(slot, bank)
                P.add("dve", lambda e, j=j, bank=bank: e.tensor_copy(out=kT[:, j, 512:1024], in_=pfb(bank)),
                      reads=[("pf", bank)], writes=[("kTc", j)])
                stream.release()
            for qt in range(4):
                bank = r_proj.next()
                for kc in range(8):
                    P.add("pe", lambda e, kc=kc, qt=qt, bank=bank: e.matmul(
                        pfb(bank), lhsT=uT[:, kc, qt * 128:(qt + 1) * 128], rhs=wvt[:, kc, :],
                        start=(kc == 0), stop=(kc == 7)),
                        reads=["wv", (("uT", qt), kc)], writes=[("pf", bank)])
                P.add("act", lambda e, qt=qt, bank=bank: e.activation(out=vt[:, 4 + qt, :], in_=pfb(bank),
                                                                      func=AF.Copy),
                      reads=[("pf", bank)], writes=[("v", 4 + qt)])
            for i in range(4):
                sa = stream.get()
                sbk = stream.get()
                ba = r_proj.next()
                bb = r_proj.next()
                proj(sa, ba)
                proj(sbk, bb)
                P.add("act", lambda e, i=i, bb=bb: e.activation(out=sB, in_=pfb(bb), func=AF.Sigmoid,
                                                                bias=cst[:, C_GLUB + i:C_GLUB + i + 1], scale=1.0),
                      reads=[("pf", bb), "cst"], writes=["sB"])
                P.add("dve", lambda e, i=i, ba=ba: e.scalar_tensor_tensor(
                    out=cg[:, i, 30:542], in0=pfb(ba), scalar=cst[:, C_GLUA + i:C_GLUA + i + 1], in1=sB,
                    op0=ALU.add, op1=ALU.mult),
                    reads=[("pf", ba), "sB", "cst"], writes=[("cg", i)])
                stream.release(2)

            def roll(first):
                P.add("sp", lambda e: e.dma_start(out=kT[:, :, 0:512], in_=kT[:, :, 512:1024]),
                      reads=[("kTc", j) for j in range(4)], writes=[("kTh", j) for j in range(4)], dma="rollk")
                P.add("sp", lambda e: e.dma_start(out=vt[:, 0:4, :], in_=vt[:, 4:8, :]),
                      reads=[("v", 4 + q) for q in range(4)], writes=[("v", q) for q in range(4)], dma="rollv")
                if first:
                    P.add("dve", lambda e: e.tensor_scalar(out=cgh_p[:], in0=cg[:, :, 512:542],
                                                           scalar1=hm[:, 1:2], scalar2=None, op0=ALU.mult),
                          reads=[("cg", i) for i in range(4)] + ["hm"], writes=["cgh_p"])
                else:
                    P.add("pool", lambda e: e.tensor_copy(out=cgh_p[:], in_=cg[:, :, 512:542]),
                          reads=[("cg", i) for i in range(4)], writes=["cgh_p"])

            if partial:
                roll(True)
                return

            P.add("pool", lambda e: e.tensor_copy(out=cg[:, :, 0:30], in_=cgh_p[:]),
                  reads=["cgh_p"], writes=[("cgh", i) for i in range(4)])

            sqb = sq.bitcast(BF16)
            r_cv = Rot(8)

            def conv_tap(k):
                for i in range(1):
                    w = cst[:, C_DWW + i * 31 + k:C_DWW + i * 31 + k + 1]
                    if k == 0:
                        P.add("act", lambda e, i=i, w=w: e.activation(
                            out=acc[:, i, :], in_=cg[:, i, 0:512], func=AF.Identity, scale=w,
                            bias=cst[:, C_DWB + i:C_DWB + i + 1]),
                            reads=[("cg", i), ("cgh", i), "cst"], writes=[("acc", i)])
                    else:
                        P.add("dve", lambda e, i=i, w=w, k=k: e.scalar_tensor_tensor(
                            out=acc[:, i, :], in0=cg[:, i, k:k + 512], scalar=w, in1=acc[:, i, :],
                            op0=ALU.mult, op1=ALU.add),
                            reads=[("cg", i), ("cgh", i), ("acc", i), "cst"], writes=[("acc", i)])

            cv_slot = {}

            def conv_act(i, k):
                w = cst[:, C_DWW + i * 31 + k:C_DWW + i * 31 + k + 1]
                cv = r_cv.next()
                cv_slot[(i, k)] = cv
                P.add("act", lambda e: e.activation(out=mT[:, cv, :], in_=cg[:, i, k:k + 512], func=AF.Copy, scale=w),
                      reads=[("cg", i), ("cgh", i), "cst"], writes=[("mT", cv)])

            def conv_mm(i, k):
                cv = cv_slot[(i, k)]
                P.add("pe", lambda e: e.matmul(pfb(5), lhsT=identb[:], rhs=mT[:, cv, :], start=(k == 0),
                                               stop=(k == 30)),
                      reads=[("mT", cv), "identb"], writes=[("pf", 5)])
                if k == 30:
                    P.add("act", lambda e: e.activation(out=acc[:, i, :], in_=pfb(5), func=AF.Identity,
                                                        bias=cst[:, C_DWB + i:C_DWB + i + 1], scale=1.0),
                          reads=[("pf", 5), "cst"], writes=[("acc", i)])

            def layernorm_silu():
                for i in range(4):
                    P.add("pe", lambda e, i=i: e.matmul(pfb(5), lhsT=ones32[:], rhs=acc[:, i, :],
                                                        start=(i == 0), stop=(i == 3)),
                          reads=[("acc", i), "ones32"], writes=[("pf", 5)])
                P.add("dve", lambda e: e.tensor_scalar(out=mean, in0=pfb(5), scalar1=1.0 / 512, scalar2=None,
                                                       op0=ALU.mult),
                      reads=[("pf", 5)], writes=["mean"])
                for i in range(4):
                    P.add("act", lambda e, i=i: e.activation(out=sq, in_=acc[:, i, :], func=AF.Square),
                          reads=[("acc", i)], writes=["sq"])
                    P.add("pe", lambda e, i=i: e.matmul(pfb(5), lhsT=ones32[:], rhs=sq,
                                                        start=(i == 0), stop=(i == 3)),
                          reads=["sq", "ones32"], writes=[("pf", 5)])
                P.add("dve", lambda e: e.tensor_tensor(out=tmpv, in0=mean, in1=mean, op=ALU.mult),
                      reads=["mean"], writes=["tmpv"])
                P.add("dve", lambda e: e.scalar_tensor_tensor(out=tmpv, in0=pfb(5), scalar=1.0 / 512, in1=tmpv,
                                                              op0=ALU.mult, op1=ALU.subtract),
                      reads=[("pf", 5), "tmpv"], writes=["tmpv"])
                P.add("dve", lambda e: e.tensor_scalar(out=tmpv, in0=tmpv, scalar1=EPS, scalar2=None, op0=ALU.add),
                      reads=["tmpv"], writes=["tmpv"])
                P.add("act", lambda e: e.activation(out=rstdt, in_=tmpv, func=AF.Sqrt),
                      reads=["tmpv"], writes=["rstdt"])
                P.add("dve", lambda e: e.reciprocal(out=rstdt, in_=rstdt), reads=["rstdt"], writes=["rstdt"])
                for i in range(4):
                    P.add("dve", lambda e, i=i: e.tensor_tensor(out=acc[:, i, :], in0=acc[:, i, :], in1=mean,
                                                                op=ALU.subtract),
                          reads=[("acc", i), "mean"], writes=[("acc", i)])
                    P.add("dve", lambda e, i=i: e.tensor_tensor(out=acc[:, i, :], in0=acc[:, i, :], in1=rstdt,
                                                                op=ALU.mult),
                          reads=[("acc", i), "rstdt"], writes=[("acc", i)])
                    P.add("act", lambda e, i=i: e.activation(
                        out=so[:, i, :], in_=acc[:, i, :], func=AF.Silu,
                        bias=cst[:, C_LNB + i:C_LNB + i + 1], scale=cst[:, C_LNG + i:C_LNG + i + 1]),
                        reads=[("acc", i), "cst"], writes=[("so", i)])

            items = [(qt, hd) for qt in range(4) for hd in range(8)]
            n_it = len(items)
            st_ = {}
            qt_rs = {}

            def stage_qk(n):
                qt, hd = items[n]
                if hd == 0:
                    qt_rs[qt] = r_o.next()
                rs = qt_rs[qt]
                j = hd // 2
                r0 = 64 * (hd % 2)
                sp_ = r_s.next()
                b0, b1 = 2 * sp_, 2 * sp_ + 1
                kreads = [("qT", j), ("kTc", j), ("kTh", j)]
                P.add("pe", lambda e: e.matmul(
                    pfb(b0), lhsT=qT[r0:r0 + 64, j, qt * 128:(qt + 1) * 128],
                    rhs=kT[r0:r0 + 64, j, 128 * qt:128 * qt + 512], start=True, stop=True),
                    reads=kreads, writes=[("pf", b0)])
                P.add("pe", lambda e: e.matmul(
                    pfb(b1, 128), lhsT=qT[r0:r0 + 64, j, qt * 128:(qt + 1) * 128],
                    rhs=kT[r0:r0 + 64, j, 128 * qt + 512:128 * qt + 640], start=True, stop=True),
                    reads=kreads, writes=[("pf", b1)])
                ss = r_ssb.next()
                s = s_sb[ss]
                P.add("dve", lambda e: e.tensor_tensor(
                    out=s[:, 0:512], in0=pfb(b0), in1=biasT[:, hd, 0:512], op=ALU.add),
                    reads=[("pf", b0), "biasT"], writes=[("s", ss, 0)])
                P.add("dve", lambda e: e.tensor_tensor(
                    out=s[:, 512:640], in0=pfb(b1, 128), in1=biasT[:, hd, 512:640], op=ALU.add),
                    reads=[("pf", b1), "biasT"], writes=[("s", ss, 1)])
                if halo_mask:
                    ncol = 512 - 128 * qt
                    P.add("dve", lambda e: e.tensor_scalar(
                        out=s[:, 0:ncol], in0=s[:, 0:ncol], scalar1=hm[:, 0:1], scalar2=None, op0=ALU.add),
                        reads=[("s", ss, 0), "hm"], writes=[("s", ss, 0)])
                a = r_ast.next()
                P.add("dve", lambda e: e.tensor_reduce(out=ast[:, a:a + 1], in_=s, axis=AX.X,
                                                       op=ALU.max, negate=True),
                      reads=[("s", ss, 0), ("s", ss, 1)], writes=[("ast", a)])
                pi = r_p.next()
                p = p_sb[pi]
                P.add("act", lambda e: e.activation(
                    out=p, in_=s, func=AF.Exp, bias=ast[:, a:a + 1], scale=1.0,
                    accum_out=rsum[:, rs, hd:hd + 1]),
                    reads=[("s", ss, 0), ("s", ss, 1), ("ast", a)], writes=[("p", pi), ("rsum", rs, hd)])
                st_[n] = dict(pi=pi, rs=rs)

            def stage_t(n):
                d = st_[n]
                pi = d["pi"]
                p = p_sb[pi]
                tb = r_pb.next()
                for jb in range(5):
                    P.add("pe", lambda e, jb=jb: e.transpose(
                        out=pbb(tb, 128, jb * 128), in_=p[:, jb * 128:(jb + 1) * 128], identity=identb[:]),
                        reads=[("p", pi), "identb"], writes=[("pb", tb)])
                ti_ = r_PT.next()
                PT = PT_sb[ti_]
                P.add("act", lambda e: e.activation(out=PT, in_=pbb(tb, 640), func=AF.Copy),
                      reads=[("pb", tb)], writes=[("PT", ti_)])
                d["ti"] = ti_

            def stage_pv(n):
                qt, hd = items[n]
                d = st_[n]
                ti_ = d["ti"]
                rs = d["rs"]
                ob = 4
                PT = PT_sb[ti_]
                for jb in range(5):
                    P.add("pe", lambda e, jb=jb: e.matmul(
                        pfb(ob, 64, hd * 64), lhsT=PT[:, jb * 128:(jb + 1) * 128],
                        rhs=vt[:, qt + jb, hd * 64:(hd + 1) * 64], start=(jb == 0), stop=(jb == 4)),
                        reads=[("PT", ti_), ("v", qt + jb)], writes=[("pf", ob)])
                if hd == 7:
                    P.add("dve", lambda e: e.reciprocal(out=rinv[:, rs, :], in_=rsum[:, rs, :]),
                          reads=[("rsum", rs, h_) for h_ in range(8)], writes=[("rinv", rs)])
                    for h_ in range(8):
                        P.add("act", lambda e, h_=h_: e.activation(
                            out=o_sb[:, h_ * 64:(h_ + 1) * 64], in_=pfb(ob, 64, h_ * 64), func=AF.Copy,
                            scale=rinv[:, rs, h_:h_ + 1]),
                            reads=[("pf", ob), ("rinv", rs)], writes=[("o_sb", h_)])
                    return qt
                return None

            def finalize(qt):
                tb = r_pb.next()
                for fc in range(4):
                    P.add("pe", lambda e, fc=fc: e.transpose(
                        out=pbb(tb, 128, fc * 128), in_=o_sb[:, fc * 128:(fc + 1) * 128], identity=identb[:]),
                        reads=[("o_sb", 2 * fc), ("o_sb", 2 * fc + 1), "identb"], writes=[("pb", tb)])
                for fc in range(4):
                    P.add("act", lambda e, fc=fc: e.activation(
                        out=attnT[:, fc, qt * 128:(qt + 1) * 128], in_=pbb(tb, 128, fc * 128), func=AF.Copy),
                        reads=[("pb", tb)], writes=[("attnT", fc, qt)])

            pending = None
            cv_list = [(i, k) for i in (1, 2, 3) for k in range(31)]
            cv_prev = []
            dv_list = list(range(31))
            ln_done = [False]
            for step in range(n_it + 3):
                for _ in range(1):
                    if dv_list:
                        conv_tap(dv_list.pop(0))
                for it in cv_prev:
                    conv_mm(*it)
                if cv_prev and not cv_list and not ln_done[0]:
                    ln_done[0] = True
                    cv_prev = []
                    layernorm_silu()
                cv_prev = []
                for _ in range(CONV_RATE):
                    if cv_list:
                        it = cv_list.pop(0)
                        conv_act(*it)
                        cv_prev.append(it)
                if step < n_it:
                    stage_qk(step)
                if 0 <= step - 1 < n_it:
                    stage_t(step - 1)
                if pending is not None:
                    finalize(pending)
                    pending = None
                if 0 <= step - 2 < n_it:
                    pending = stage_pv(step - 2)
            assert ln_done[0]

            for dc in range(8):
                sa = stream.get()
                sbk = stream.get()
                proj(sa, 0)
                proj(sbk, 1)
                for fc in range(4):
                    P.add("pe", lambda e, fc=fc, dc=dc: e.matmul(
                        pfb(2), lhsT=waot[:, fc, dc * 128:(dc + 1) * 128], rhs=attnT[:, fc, :],
                        start=(fc == 0), stop=(fc == 3)),
                        reads=["wao"] + [("attnT", fc, q) for q in range(4)], writes=[("pf", 2)])
                for fc in range(4):
                    P.add("pe", lambda e, fc=fc, dc=dc: e.matmul(
                        pfb(3), lhsT=wcot[:, fc, dc * 128:(dc + 1) * 128], rhs=so[:, fc, :],
                        start=(fc == 0), stop=(fc == 3)),
                        reads=["wco", ("so", fc)], writes=[("pf", 3)])
                P.add("act", lambda e, dc=dc: e.activation(out=sA, in_=pfb(0), func=AF.Sigmoid,
                                                           bias=cst[:, C_GATEB + dc:C_GATEB + dc + 1], scale=1.0),
                      reads=[("pf", 0), "cst"], writes=["sA"])
                P.add("act", lambda e, dc=dc: e.activation(out=sB, in_=pfb(1), func=AF.Sigmoid,
                                                           bias=cst[:, C_GATEB + 8 + dc:C_GATEB + 9 + dc], scale=1.0),
                      reads=[("pf", 1), "cst"], writes=["sB"])
                P.add("dve", lambda e: e.tensor_tensor(out=sA, in0=sA, in1=pfb(2), op=ALU.mult),
                      reads=["sA", ("pf", 2)], writes=["sA"])
                P.add("dve", lambda e: e.tensor_tensor(out=sB, in0=sB, in1=pfb(3), op=ALU.mult),
                      reads=["sB", ("pf", 3)], writes=["sB"])
                P.add("dve", lambda e, dc=dc: e.tensor_tensor(out=mT[:, dc, :], in0=sA, in1=sB, op=ALU.add),
                      reads=["sA", "sB"], writes=[("mT", dc)])
                stream.release(2)
            for qt, hi in enumerate(htiles):
                fb = r_f.next()
                banks = (2 * fb, 2 * fb + 1)
                for half in range(2):
                    for kc in range(8):
                        P.add("pe", lambda e, kc=kc, half=half, qt=qt, banks=banks: e.matmul(
                            pfb(banks[half]), lhsT=mT[:, kc, qt * 128:(qt + 1) * 128],
                            rhs=woutt[:, kc, half * 512:(half + 1) * 512], start=(kc == 0), stop=(kc == 7)),
                            reads=["wout", ("mT", kc)], writes=[("pf", banks[half])])
                postnorm(hi, banks, 1.0, ptmp_m)
            roll(False)

        ffn(0, [0, 1, 2, 3], xh, 0, None, 0)
        barrier()
        mixer([0, 1, 2, 3], True, False)
        for g in range(ng):
            ffn(0, list(range(8)), xo, g * 1024, None, 0, pre_done=(g > 0))
            barrier()
            mixer([0, 1, 2, 3], False, g == 0, late_loads=True)
            mixer([4, 5, 6, 7], False, False)
            ffn(1, list(range(8)), None, 0, y_d, g * 1024,
                fuse_next=((xo, (g + 1) * 1024) if g + 1 < ng else None))
        P.add("sp", None, reads=[("y", i) for i in range(8)])
        P.emit(block, sems, dsems)
    return nc


_NC_CACHE = {}


def _prep_shared(inp):
    f = lambda a: np.ascontiguousarray(np.asarray(a, dtype=np.float32))

    def wgu(wg, wu):
        g = f(wg)[0].reshape(8, 128, NCH, 128).transpose(2, 1, 0, 3)
        u = f(wu)[0].reshape(8, 128, NCH, 128).transpose(2, 1, 0, 3)
        return np.ascontiguousarray(np.stack([g, u], axis=2).reshape(NCH, 128, 2048))

    def wdl(wd):
        return np.ascontiguousarray(f(wd)[0].reshape(NCH, 128, 1024))

    def rows(w, nk):
        n = w.shape[1]
        return np.ascontiguousarray(w.reshape(nk, 128, n).transpose(1, 0, 2).reshape(128, nk * n))

    w_in = f(inp["w_in"])[0]
    winb = np.ascontiguousarray(w_in.reshape(8, 128, 36, 128).transpose(2, 1, 0, 3).reshape(36, 128, 1024))
    wv = rows(w_in[:, 1024:1536], 8)
    wao = rows(f(inp["w_attn_out"])[0], 4)
    wco = rows(f(inp["conv_w_out"])[0], 4)
    wout = rows(f(inp["w_out"])[0], 8)
    rel = f(inp["rel_table"])[0]
    r = np.arange(128)[:, None]
    j = np.arange(640)[None, :]
    idx = np.clip(r + 512 - j, -128, 128) + 128
    vis = ((j // 64) >= (r // 64)) & ((j // 64) <= (r // 64) + 8)
    bias = rel[:, idx]
    bias = np.where(vis[None], bias, np.float32(-1e30)).astype(np.float32)
    biasT = np.ascontiguousarray(bias.transpose(1, 0, 2).reshape(128, 5120))

    def cols(v, n):
        return f(v).reshape(n, 128).T

    cst = np.zeros((128, NCST), np.float32)
    cst[:, C_GPRE1:C_GPRE1 + 8] = cols(inp["ffn1_norm_pre"][0], 8)
    cst[:, C_GPREM:C_GPREM + 8] = cols(inp["mix_norm_pre"][0], 8)
    cst[:, C_GPRE2:C_GPRE2 + 8] = cols(inp["ffn2_norm_pre"][0], 8)
    cst[:, C_GATEB:C_GATEB + 16] = cols(inp["gate_bias"][0], 16)
    glu = f(inp["conv_glu_bias"])[0]
    cst[:, C_GLUA:C_GLUA + 4] = cols(glu[:512], 4)
    cst[:, C_GLUB:C_GLUB + 4] = cols(glu[512:], 4)
    dw = f(inp["conv_dw_w"])[0][:, 0, :]
    cst[:, C_DWW:C_DWW + 124] = dw.reshape(31, 4, 128).transpose(2, 1, 0).reshape(128, 124)
    cst[:, C_DWB:C_DWB + 4] = cols(inp["conv_dw_b"][0], 4)
    cst[:, C_LNG:C_LNG + 4] = cols(inp["conv_ln_g"][0], 4)
    cst[:, C_LNB:C_LNB + 4] = cols(inp["conv_ln_b"][0], 4)
    gpost = np.stack([np.broadcast_to(f(inp[k])[0][None, :], (128, 1024))
                      for k in ("ffn1_norm_post", "mix_norm_post", "ffn2_norm_post")])
    return {
        "wgu1": wgu(inp["ffn1_w_gate"], inp["ffn1_w_up"]),
        "wgu2": wgu(inp["ffn2_w_gate"], inp["ffn2_w_up"]),
        "wd1": wdl(inp["ffn1_w_down"]),
        "wd2": wdl(inp["ffn2_w_down"]),
        "winb": winb, "wv": wv, "wao": wao, "wco": wco, "wout": wout,
        "biasT": biasT, "cst": cst, "gpost": np.ascontiguousarray(gpost),
        "ident": np.eye(128, dtype=np.float32),
    }


def kernel(**inputs):
    x = np.asarray(inputs["x"], dtype=np.float32)
    shared = _prep_shared(inputs)
    if "nc" not in _NC_CACHE:
        _NC_CACHE["nc"] = build_nc()
    nc = _NC_CACHE["nc"]
    in_maps = []
    for core in range(NCORES):
        b, half = core // 2, core % 2
        m = dict(shared)
        m["xo"] = np.ascontiguousarray(x[b, half * TOK:(half + 1) * TOK])
        hmv = np.zeros((128, 2), np.float32)
        if half == 0:
            m["xh"] = np.zeros((HALO, D), np.float32)
            hmv[:, 0] = -1e30
            hmv[:, 1] = 0.0
        else:
            m["xh"] = np.ascontiguousarray(x[b, TOK - HALO:TOK])
            hmv[:, 0] = 0.0
            hmv[:, 1] = 1.0
        m["hm"] = hmv
        in_maps.append(m)
    res = run_bass_kernel_spmd(nc, in_maps, core_ids=list(range(NCORES)))
    out = np.empty((4, 2 * TOK, D), np.float32)
    for core in range(NCORES):
        b, half = core // 2, core % 2
        out[b, half * TOK:(half + 1) * TOK] = res.results[core]["y"]
    return out
```

```python
import numpy as np
from contextlib import ExitStack
import concourse.bass as bass
import concourse.mybir as mybir
from concourse.bass_utils import run_bass_kernel_spmd

F32 = mybir.dt.float32
BF16 = mybir.dt.bfloat16
ALU = mybir.AluOpType
AF = mybir.ActivationFunctionType
AX = mybir.AxisListType

ENG = ("pe", "act", "dve", "pool", "sp")
EPS = 1e-6
D = 1024
DFF = 2816
NCH = 22
NCORES = 8
TOK = 4096
HALO = 512

C_GPRE1, C_GPREM, C_GPRE2 = 0, 8, 16
C_GATEB = 24
C_GLUA, C_GLUB = 40, 44
C_DWW = 48
C_DWB = 48 + 124
C_LNG = C_DWB + 4
C_LNB = C_LNG + 4
NCST = C_LNB + 4
MAX_OPS = None
EVAC_ACT = (0,)


class Op:
    __slots__ = ("eng", "fn", "dma", "pos", "flag", "waits", "dma_target", "done_clock")


class Prog:
    def __init__(self):
        self.ops = {e: [] for e in ENG}
        self.comp = {e: [] for e in ENG}
        self.last_w = {}
        self.readers = {}
        self.dma_count = {}
        self.eng_clock = {e: {} for e in ENG}
        self.phase_key = "phase"

    def add(self, eng, fn, reads=(), writes=(), dma=None):
        self.n_added = getattr(self, "n_added", 0) + 1
        if MAX_OPS is not None and self.n_added > MAX_OPS:
            return None
        op = Op()
        op.eng = eng
        op.fn = fn
        op.dma = dma
        op.flag = False
        reads = list(reads)
        reads.append(self.phase_key)
        deps = []
        lw = self.last_w
        rd = self.readers
        for k in reads:
            d = lw.get(k)
            if d is not None:
                deps.append(d)
            if isinstance(k, tuple) and (k[0] == "pf" or k[0] == "pb"):
                r = rd.get(k)
                if r:
                    deps.extend(x for x in r if x.eng != eng)
        for k in writes:
            d = lw.get(k)
            if d is not None:
                deps.append(d)
            r = rd.get(k)
            if r:
                deps.extend(r)
        clock = self.eng_clock[eng]
        waits = {}
        pe_self = (eng == "pe" and dma is None)
        for d in deps:
            if d.dma is not None:
                dim = ("dma", d.dma)
                val = d.dma_target
            else:
                if pe_self and d.eng == "pe":
                    continue
                dim = d.eng
                val = d.pos
            if clock.get(dim, 0) >= val:
                continue
            if waits.get(dim, 0) < val:
                waits[dim] = val
        for d in deps:
            if d.dma is None and pe_self and d.eng == "pe":
                continue
            for dim, val in d.done_clock.items():
                if clock.get(dim, 0) < val:
                    clock[dim] = val
        for dim, val in waits.items():
            if clock.get(dim, 0) < val:
                clock[dim] = val
            if not isinstance(dim, tuple):
                self.comp[dim][val - 1].flag = True
        op.waits = waits
        if fn is None:
            op.pos = None
            op.done_clock = None
            self.ops[eng].append(op)
            return op
        dc = dict(clock)
        if dma is not None:
            c = self.dma_count.get(dma, 0) + 1
            self.dma_count[dma] = c
            op.dma_target = c
            op.pos = None
            dc[("dma", dma)] = c
        else:
            self.comp[eng].append(op)
            op.pos = len(self.comp[eng])
            dc[eng] = op.pos
        op.done_clock = dc
        self.ops[eng].append(op)
        for k in reads:
            rd.setdefault(k, []).append(op)
        for k in writes:
            lw[k] = op
            rd[k] = []
        return op

    def emit(self, block, eng_sems, dma_sems):
        counts = {}
        for e in ENG:
            c = 0
            arr = []
            for op in self.comp[e]:
                if op.flag:
                    c += 1
                arr.append(c)
            counts[e] = arr

        def run(e, engine):
            for op in self.ops[e]:
                for dim, val in op.waits.items():
                    if isinstance(dim, tuple):
                        engine.wait_ge(dma_sems[dim[1]], 16 * val)
                    else:
                        engine.wait_ge(eng_sems[dim], counts[dim][val - 1])
                if op.fn is None:
                    continue
                ins = op.fn(engine)
                if op.dma is not None:
                    ins.then_inc(dma_sems[op.dma], 16)
                elif op.flag:
                    ins.then_inc(eng_sems[e], 1)

        block.tensor(lambda eng: run("pe", eng))
        block.scalar(lambda eng: run("act", eng))
        block.vector(lambda eng: run("dve", eng))
        block.gpsimd(lambda eng: run("pool", eng))
        block.sync(lambda eng: run("sp", eng))


class Rot:
    def __init__(self, n):
        self.n = n
        self.i = 0

    def next(self):
        v = self.i
        self.i = (self.i + 1) % self.n
        return v


def build_nc(ng=4):
    nc = bass.Bass("TRN2", target_bir_lowering=False)

    def din(name, shape):
        return nc.dram_tensor(name, shape, F32, kind="ExternalInput").ap()

    xo = din("xo", [TOK, D])
    xh = din("xh", [HALO, D])
    hm_d = din("hm", [128, 2])
    wgu_d = [din("wgu1", [NCH, 128, 2048]), din("wgu2", [NCH, 128, 2048])]
    wd_d = [din("wd1", [NCH, 128, 1024]), din("wd2", [NCH, 128, 1024])]
    winb_d = din("winb", [36, 128, 1024])
    wv_d = din("wv", [128, 4096])
    wao_d = din("wao", [128, 4096])
    wco_d = din("wco", [128, 4096])
    wout_d = din("wout", [128, 8192])
    bias_d = din("biasT", [128, 5120])
    cst_d = din("cst", [128, NCST])
    gpost_d = din("gpost", [3, 128, 1024])
    ident_d = din("ident", [128, 128])
    y_d = nc.dram_tensor("y", [TOK, D], F32, kind="ExternalOutput").ap()

    P = Prog()
    global LASTP
    LASTP = P
    es = ExitStack()
    with es:
        def sb(name, shape, dt):
            return es.enter_context(nc.sbuf_tensor(name, shape, dt))

        h = sb("h", [128, 8, 1024], F32)
        kT = sb("kT", [128, 4, 1024], BF16)
        vt = sb("vt", [128, 8, 512], BF16)
        gp = sb("gp", [128, 1024], F32)
        junk = sb("junk", [128, 2, 1024], BF16)
        mhalf = sb("mhalf", [128, 512], F32)
        st = sb("st", [128, 64], F32)
        ast = sb("ast", [128, 16], F32)
        rsum = sb("rsum", [128, 2, 8], F32)
        rinv = sb("rinv", [128, 2, 8], F32)
        cst = sb("cst_sb", [128, NCST], F32)
        identf = sb("identf", [128, 128], F32)
        identb = sb("identb", [128, 128], BF16)
        ones32 = sb("ones32", [128, 128], F32)
        hm = sb("hm_sb", [128, 2], F32)
        bscr = sb("bscr", [128, 2], F32)
        cgh_p = sb("cgh_p", [128, 4, 30], F32)
        ARENA_WORDS = 36800
        arena = sb("arena", [128, ARENA_WORDS], F32)
        pf = es.enter_context(nc.psum_tensor("pf", [128, 3072], F32))
        pb = es.enter_context(nc.psum_tensor("pb", [128, 2048], BF16))

        class Arena:
            def __init__(self):
                self.off = 0

            def take(self, free_shape, dt):
                nel = int(np.prod(free_shape))
                words = nel if dt == F32 else (nel + 1) // 2
                assert self.off + words <= ARENA_WORDS, (self.off, words)
                ap = arena[:, self.off:self.off + words]
                if dt == BF16:
                    ap = ap.bitcast(BF16)
                self.off += words
                if len(free_shape) == 2:
                    ap = ap.rearrange("p (a b) -> p a b", a=free_shape[0])
                elif len(free_shape) == 3:
                    ap = ap.rearrange("p (a b c) -> p a b c", a=free_shape[0], b=free_shape[1])
                return ap

        A = Arena()
        xnT = A.take((8, 1024), BF16)
        hT = A.take((NCH, 1024), BF16)
        wdt = A.take((NCH, 1024), BF16)
        NSG = 4
        wgus = [A.take((2, 8, 128), BF16) for _ in range(NSG)]
        sg = [A.take((512,), F32) for _ in range(2)]
        xn_f = [A.take((1024,), BF16) for _ in range(2)]
        ptmp_f = [A.take((512,), F32) for _ in range(2)]
        M = Arena()
        uT = M.take((8, 512), BF16)
        qT = M.take((4, 512), BF16)
        biasT = M.take((8, 640), F32)
        cg0 = M.take((542,), F32)
        cgb = M.take((3, 544), BF16)
        acc = M.take((4, 512), F32)
        sq = M.take((512,), F32)
        mean = M.take((512,), F32)
        tmpv = M.take((512,), F32)
        rstdt = M.take((512,), F32)
        so = M.take((4, 512), BF16)
        attnT = M.take((4, 512), BF16)
        mT = M.take((8, 512), BF16)
        s_sb = [M.take((640,), F32) for _ in range(2)]
        p_sb = [M.take((640,), BF16) for _ in range(2)]
        PT_sb = [M.take((640,), BF16) for _ in range(2)]
        o_sb = M.take((512,), BF16)
        sA = M.take((512,), F32)
        sB = M.take((512,), F32)
        NSW = 4
        wsl = [M.take((8, 128), BF16) for _ in range(NSW)]
        wvt = M.take((8, 512), BF16)
        waot = M.take((4, 1024), BF16)
        wcot = M.take((4, 1024), BF16)
        woutt = M.take((8, 1024), BF16)
        xn_m = [M.take((1024,), BF16) for _ in range(2)]
        ptmp_m = [M.take((512,), F32) for _ in range(2)]

        sems = {e: es.enter_context(nc.semaphore("s_" + e)) for e in ENG}
        dkeys = (["h%d" % i for i in range(8)] + ["o%d" % i for i in range(8)] +
                 ["wgu%d" % i for i in range(NSG)] + ["wd", "gp", "cst", "id", "hm", "bias",
                                                      "wv", "wao", "wco", "wout"] +
                 ["ws%d" % i for i in range(NSW)] + ["rollk", "rollv"])
        dsems = {k: es.enter_context(nc.semaphore("d_" + k)) for k in dkeys}
        block = es.enter_context(nc.Block())

        def pfb(b, n=512, o=0):
            return pf[:, b * 512 + o:b * 512 + o + n]

        def pbb(b, n, o=0):
            return pb[:, b * 1024 + o:b * 1024 + o + n]

        r_st = Rot(16)
        r_junk = Rot(2)
        r_pb = Rot(2)
        r_xn = Rot(2)
        r_pt = Rot(2)
        r_g = Rot(2)
        r_u = Rot(2)
        r_sg = Rot(2)
        r_f = Rot(3)
        r_proj = Rot(4)
        r_s = Rot(2)
        r_ssb = Rot(2)
        r_p = Rot(2)
        r_PT = Rot(2)
        r_ast = Rot(16)
        r_o = Rot(2)
        r_alt = Rot(2)
        barrier_n = [0]

        def barrier():
            barrier_n[0] += 1
            old = P.phase_key
            op_fn = lambda e: e.memset(bscr[:, 0:1], 0.0)
            P.phase_key = "nophase"
            P.add("dve", op_fn, reads=[], writes=[old])
            P.phase_key = old

        P.add("sp", lambda e: e.dma_start(out=cst[:], in_=cst_d), writes=["cst"], dma="cst")
        P.add("sp", lambda e: e.dma_start(out=identf[:], in_=ident_d), writes=["identf"], dma="id")
        P.add("sp", lambda e: e.dma_start(out=hm[:], in_=hm_d), writes=["hm"], dma="hm")
        P.add("dve", lambda e: e.tensor_copy(out=identb[:], in_=identf[:]), reads=["identf"], writes=["identb"])
        P.add("pool", lambda e: e.memset(mhalf[:], -0.5), writes=["mhalf"])
        P.add("pool", lambda e: e.memset(ones32[:], 1.0), writes=["ones32"])

        def rstd_from(srcs, reads):
            s = r_st.next()
            key = ("st", s)
            n = srcs[0].shape[-1]
            for idx, src in enumerate(srcs):
                jk = r_junk.next()
                P.add("act", lambda e, src=src, c=4 * s + idx, jk=jk: e.activation(
                    out=junk[:, jk, 0:n], in_=src, func=AF.Square, accum_out=st[:, c:c + 1]),
                    reads=reads, writes=[(key, idx), ("junk", jk)])
            if len(srcs) == 2:
                P.add("dve", lambda e, s=s: e.tensor_tensor(
                    out=st[:, 4 * s:4 * s + 1], in0=st[:, 4 * s:4 * s + 1], in1=st[:, 4 * s + 1:4 * s + 2],
                    op=ALU.add),
                    reads=[(key, 0), (key, 1)], writes=[(key, 0)])
            P.add("dve", lambda e, s=s: e.tensor_scalar(
                out=st[:, 4 * s + 2:4 * s + 3], in0=st[:, 4 * s:4 * s + 1], scalar1=1.0 / 1024,
                scalar2=EPS, op0=ALU.mult, op1=ALU.add),
                reads=[(key, 0), (key, 1)], writes=[(key, 2)])
            P.add("pool", lambda e, s=s: e.tensor_tensor(
                out=st[:, 4 * s + 3:4 * s + 4], in0=st[:, 4 * s + 2:4 * s + 3], in1=mhalf[:, 0:1], op=ALU.pow),
                reads=[(key, 2), "mhalf"], writes=[(key, 3)])
            return st[:, 4 * s + 3:4 * s + 4], (key, 3)

        def prenorm_stats(hi):
            return rstd_from([h[:, hi, :]], [("h", hi)])

        def prenorm_apply(hi, rr, xn_bufs, dst, dstkey, col0, g0):
            r, rkey = rr
            xs = r_xn.next()
            xn = xn_bufs[xs]
            P.add("dve", lambda e: e.tensor_scalar(out=xn, in0=h[:, hi, :], scalar1=r, scalar2=None,
                                                   op0=ALU.mult),
                  reads=[("h", hi), rkey], writes=[("xn", xs)])
            tb = r_pb.next()
            for c in range(8):
                P.add("pe", lambda e, c=c: e.transpose(out=pbb(tb, 128, c * 128), in_=xn[:, c * 128:(c + 1) * 128],
                                                       identity=identb[:]),
                      reads=[("xn", xs), "identb"], writes=[("pb", tb)])
            for c in range(8):
                o = dst[:, c, col0:col0 + 128]
                i_ = pbb(tb, 128, c * 128)
                g = cst[:, g0 + c:g0 + c + 1]
                if tb == 0:
                    P.add("act", lambda e, o=o, i_=i_, g=g: e.activation(out=o, in_=i_, func=AF.Copy, scale=g),
                          reads=[("pb", tb), "cst"], writes=[(dstkey, c)])
                else:
                    P.add("dve", lambda e, o=o, i_=i_, g=g: e.tensor_scalar(out=o, in0=i_, scalar1=g, scalar2=None,
                                                                         op0=ALU.mult),
                          reads=[("pb", tb), "cst"], writes=[(dstkey, c)])

        def prenorm_tile(hi, xn_bufs, dst, dstkey, col0, g0):
            prenorm_apply(hi, prenorm_stats(hi), xn_bufs, dst, dstkey, col0, g0)

        def postnorm(hi, banks, factor, ptmp):
            r, rkey = rstd_from([pfb(banks[0]), pfb(banks[1])], [("pf", banks[0]), ("pf", banks[1])])
            for half in range(2):
                k = r_pt.next()
                tmp = ptmp[k]
                b = banks[half]
                P.add("dve", lambda e, tmp=tmp, b=b, half=half: e.scalar_tensor_tensor(
                    out=tmp, in0=pfb(b), scalar=r, in1=gp[:, half * 512:(half + 1) * 512],
                    op0=ALU.mult, op1=ALU.mult),
                    reads=[("pf", b), rkey, "gp"], writes=[("ptmp", k)])
                P.add("dve", lambda e, tmp=tmp, half=half: e.scalar_tensor_tensor(
                    out=h[:, hi, half * 512:(half + 1) * 512], in0=tmp, scalar=float(factor),
                    in1=h[:, hi, half * 512:(half + 1) * 512], op0=ALU.mult, op1=ALU.add),
                    reads=[("ptmp", k), ("h", hi)], writes=[("h", hi)])

        def ffn(which, htiles, xsrc, xoff, outdst, ooff):
            nt = len(htiles)
            nsub = nt // 4
            g0 = C_GPRE1 if which == 0 else C_GPRE2
            barrier()
            P.add("sp", lambda e: e.dma_start(out=gp[:], in_=gpost_d[0 if which == 0 else 2]),
                  writes=["gp"], dma="gp")

            def load_chunk(c):
                slot = c % NSG
                P.add("pool", lambda e: e.dma_start(
                    out=wgus[slot].rearrange("p a b c -> p (a b c)"), in_=wgu_d[which][c]),
                    writes=[("wgu", slot)], dma="wgu%d" % slot)
                P.add("pool", lambda e: e.dma_start(out=wdt[:, c, :], in_=wd_d[which][c]),
                      writes=[("wd", c)], dma="wd")

            rrs = []
            for ti, hi in enumerate(htiles):
                if xsrc is not None:
                    P.add("sp", lambda e, ti=ti, hi=hi: e.dma_start(
                        out=h[:, hi, :], in_=xsrc[xoff + ti * 128:xoff + (ti + 1) * 128, :]),
                        writes=[("h", hi)], dma="h%d" % hi)
            for ti, hi in enumerate(htiles):
                rrs.append(prenorm_stats(hi))
            for c in range(NSG):
                load_chunk(c)
            for ti, hi in enumerate(htiles):
                prenorm_apply(hi, rrs[ti], xn_f, xnT, ("xnT", ti), ti * 128, g0)
            if nsub == 2:
                units = [(0, 0), (1, 0), (2, 0), (0, 1), (1, 1), (2, 1)] + \
                        [(c, sb_) for c in range(3, NCH) for sb_ in range(2)]
            else:
                units = [(c, 0) for c in range(NCH)]
            done_cnt = {}
            for (c, sub) in units:
                slot = c % NSG
                if True:
                    gb = r_g.next()
                    ub = 2 + r_u.next()
                    for kc in range(8):
                        P.add("pe", lambda e, kc=kc, gb=gb, sub=sub, slot=slot: e.matmul(
                            pfb(gb), lhsT=wgus[slot][:, 0, kc, :], rhs=xnT[:, kc, sub * 512:(sub + 1) * 512],
                            start=(kc == 0), stop=(kc == 7)),
                            reads=[("wgu", slot)] + [(("xnT", t), kc) for t in range(4 * sub, 4 * sub + 4)],
                            writes=[("pf", gb)])
                    for kc in range(8):
                        P.add("pe", lambda e, kc=kc, ub=ub, sub=sub, slot=slot: e.matmul(
                            pfb(ub), lhsT=wgus[slot][:, 1, kc, :], rhs=xnT[:, kc, sub * 512:(sub + 1) * 512],
                            start=(kc == 0), stop=(kc == 7)),
                            reads=[("wgu", slot)] + [(("xnT", t), kc) for t in range(4 * sub, 4 * sub + 4)],
                            writes=[("pf", ub)])
                    sgs = r_sg.next()
                    P.add("act", lambda e, gb=gb, sgs=sgs: e.activation(out=sg[sgs], in_=pfb(gb), func=AF.Silu),
                          reads=[("pf", gb)], writes=[("sg", sgs)])
                    P.add("dve", lambda e, ub=ub, sgs=sgs, sub=sub, c=c: e.tensor_tensor(
                        out=hT[:, c, sub * 512:(sub + 1) * 512], in0=sg[sgs], in1=pfb(ub), op=ALU.mult),
                        reads=[("sg", sgs), ("pf", ub)], writes=[("hT", c, sub)])
                done_cnt[c] = done_cnt.get(c, 0) + 1
                if done_cnt[c] == nsub and c + NSG < NCH:
                    load_chunk(c + NSG)
            P.add("pe", None, reads=[("wd", c) for c in range(NCH)])
            for ti, hi in enumerate(htiles):
                sub = ti // 4
                fb = r_f.next()
                banks = (2 * fb, 2 * fb + 1)
                for half in range(2):
                    for c in range(NCH):
                        P.add("pe", lambda e, c=c, half=half, ti=ti, banks=banks: e.matmul(
                            pfb(banks[half]), lhsT=hT[:, c, ti * 128:(ti + 1) * 128],
                            rhs=wdt[:, c, half * 512:(half + 1) * 512], start=(c == 0), stop=(c == NCH - 1)),
                            reads=[("hT", c, sub), ("wd", c)], writes=[("pf", banks[half])])
                postnorm(hi, banks, 0.5, ptmp_f)
                if outdst is not None:
                    P.add("sp", lambda e, ti=ti, hi=hi: e.dma_start(
                        out=outdst[ooff + ti * 128:ooff + (ti + 1) * 128, :], in_=h[:, hi, :]),
                        reads=[("h", hi)], writes=[("y", hi)], dma="o%d" % hi)

        class Stream:
            def __init__(self, blocks):
                self.blocks = blocks
                self.issued = 0
                self.cons = 0
                for _ in range(min(NSW, len(blocks))):
                    self._issue()

            def _issue(self):
                n = self.issued
                slot = n % NSW
                b = self.blocks[n]
                P.add("pool", lambda e: e.dma_start(out=wsl[slot].rearrange("p a b -> p (a b)"), in_=winb_d[b]),
                      writes=[("ws", slot)], dma="ws%d" % slot)
                self.issued += 1

            def get(self):
                slot = self.cons % NSW
                self.cons += 1
                return slot

            def release(self, count=1):
                for _ in range(count):
                    if self.issued < len(self.blocks):
                        self._issue()

        def mixer_load_weights(partial):
            P.add("pool", lambda e: e.dma_start(out=wvt.rearrange("p a b -> p (a b)"), in_=wv_d),
                  writes=["wv"], dma="wv")

        def mixer_load_late():
            P.add("sp", lambda e: e.dma_start(out=biasT.rearrange("p a b -> p (a b)"), in_=bias_d),
                  writes=["biasT"], dma="bias")
            P.add("sp", lambda e: e.dma_start(out=gp[:], in_=gpost_d[1]), writes=["gp"], dma="gp")
            P.add("pool", lambda e: e.dma_start(out=waot.rearrange("p a b -> p (a b)"), in_=wao_d),
                  writes=["wao"], dma="wao")
            P.add("pool", lambda e: e.dma_start(out=wcot.rearrange("p a b -> p (a b)"), in_=wco_d),
                  writes=["wco"], dma="wco")
            P.add("pool", lambda e: e.dma_start(out=woutt.rearrange("p a b -> p (a b)"), in_=wout_d),
                  writes=["wout"], dma="wout")

        def mixer(htiles, partial, halo_mask, first=False):
            blocks = [0, 1, 2, 3, 4, 5, 6, 7, 12, 16, 13, 17, 14, 18, 15, 19]
            if not partial:
                for dc in range(8):
                    blocks += [20 + dc, 28 + dc]
            rrs = [prenorm_stats(hi) for hi in htiles]
            stream = Stream(blocks)
            if first:
                mixer_load_weights(partial)
            for ti, hi in enumerate(htiles):
                prenorm_apply(hi, rrs[ti], xn_m, uT, ("uT", ti), ti * 128, C_GPREM)
            if first and not partial:
                mixer_load_late()
            uT_reads = [(("uT", t), kc) for t in range(4) for kc in range(8)]

            def proj(slot, bank):
                for kc in range(8):
                    P.add("pe", lambda e, kc=kc: e.matmul(pfb(bank), lhsT=wsl[slot][:, kc, :], rhs=uT[:, kc, :],
                                                          start=(kc == 0), stop=(kc == 7)),
                          reads=[("ws", slot)] + [(("uT", t), kc) for t in range(4)], writes=[("pf", bank)])

            for j in range(4):
                slot = stream.get()
                bank = r_proj.next()
                proj(slot, bank)
                P.add("act", lambda e, j=j, bank=bank: e.activation(out=qT[:, j, :], in_=pfb(bank), func=AF.Copy,
                                                                    scale=0.125),
                      reads=[("pf", bank)], writes=[("qT", j)])
                stream.release()
            for j in range(4):
                slot = stream.get()
                bank = r_proj.next()
                proj(slot, bank)
                P.add("dve", lambda e, j=j, bank=bank: e.tensor_copy(out=kT[:, j, 512:1024], in_=pfb(bank)),
                      reads=[("pf", bank)], writes=[("kTc", j)])
                stream.release()
            for qt in range(4):
                bank = r_proj.next()
                for kc in range(8):
                    P.add("pe", lambda e, kc=kc, qt=qt, bank=bank: e.matmul(
                        pfb(bank), lhsT=uT[:, kc, qt * 128:(qt + 1) * 128], rhs=wvt[:, kc, :],
                        start=(kc == 0), stop=(kc == 7)),
                        reads=["wv", (("uT", qt), kc)], writes=[("pf", bank)])
                P.add("act", lambda e, qt=qt, bank=bank: e.activation(out=vt[:, 4 + qt, :], in_=pfb(bank),
                                                                      func=AF.Copy),
                      reads=[("pf", bank)], writes=[("v", 4 + qt)])
            for i in range(4):
                sa = stream.get()
                sbk = stream.get()
                ba = r_proj.next()
                bb = r_proj.next()
                proj(sa, ba)
                proj(sbk, bb)
                P.add("act", lambda e, i=i, bb=bb: e.activation(out=sB, in_=pfb(bb), func=AF.Sigmoid,
                                                                bias=cst[:, C_GLUB + i:C_GLUB + i + 1], scale=1.0),
                      reads=[("pf", bb), "cst"], writes=["sB"])
                cgo = cg0[:, 30:542] if i == 0 else cgb[:, i - 1, 30:542]
                P.add("dve", lambda e, i=i, ba=ba, cgo=cgo: e.scalar_tensor_tensor(
                    out=cgo, in0=pfb(ba), scalar=cst[:, C_GLUA + i:C_GLUA + i + 1], in1=sB,
                    op0=ALU.add, op1=ALU.mult),
                    reads=[("pf", ba), "sB", "cst"], writes=[("cg", i)])
                stream.release(2)

            def roll(first):
                P.add("sp", lambda e: e.dma_start(out=kT[:, :, 0:512], in_=kT[:, :, 512:1024]),
                      reads=[("kTc", j) for j in range(4)], writes=[("kTh", j) for j in range(4)], dma="rollk")
                P.add("sp", lambda e: e.dma_start(out=vt[:, 0:4, :], in_=vt[:, 4:8, :]),
                      reads=[("v", 4 + q) for q in range(4)], writes=[("v", q) for q in range(4)], dma="rollv")
                if first:
                    P.add("dve", lambda e: e.tensor_scalar(out=cgh_p[:, 0, :], in0=cg0[:, 512:542],
                                                           scalar1=hm[:, 1:2], scalar2=None, op0=ALU.mult),
                          reads=[("cg", 0), "hm"], writes=[("cgh_p", 0)])
                    P.add("dve", lambda e: e.tensor_scalar(out=cgh_p[:, 1:4, :], in0=cgb[:, :, 512:542],
                                                           scalar1=hm[:, 1:2], scalar2=None, op0=ALU.mult),
                          reads=[("cg", i) for i in range(1, 4)] + ["hm"], writes=[("cgh_p", 1)])
                else:
                    P.add("pool", lambda e: e.tensor_copy(out=cgh_p[:, 0, :], in_=cg0[:, 512:542]),
                          reads=[("cg", 0)], writes=[("cgh_p", 0)])
                    P.add("pool", lambda e: e.tensor_copy(out=cgh_p[:, 1:4, :], in_=cgb[:, :, 512:542]),
                          reads=[("cg", i) for i in range(1, 4)], writes=[("cgh_p", 1)])

            if partial:
                roll(True)
                return

            P.add("pool", lambda e: e.tensor_copy(out=cg0[:, 0:30], in_=cgh_p[:, 0, :]),
                  reads=[("cgh_p", 0)], writes=[("cgh", 0)])
            P.add("pool", lambda e: e.tensor_copy(out=cgb[:, :, 0:30], in_=cgh_p[:, 1:4, :]),
                  reads=[("cgh_p", 1)], writes=[("cgh", i) for i in range(1, 4)])

            sqb = sq.bitcast(BF16)
            r_cv = Rot(8)

            def conv_tap(k):
                for i in range(1):
                    w = cst[:, C_DWW + i * 31 + k:C_DWW + i * 31 + k + 1]
                    if k == 0:
                        P.add("act", lambda e, i=i, w=w: e.activation(
                            out=acc[:, i, :], in_=cg0[:, 0:512], func=AF.Identity, scale=w,
                            bias=cst[:, C_DWB + i:C_DWB + i + 1]),
                            reads=[("cg", i), ("cgh", i), "cst"], writes=[("acc", i)])
                    else:
                        P.add("dve", lambda e, i=i, w=w, k=k: e.scalar_tensor_tensor(
                            out=acc[:, i, :], in0=cg0[:, k:k + 512], scalar=w, in1=acc[:, i, :],
                            op0=ALU.mult, op1=ALU.add),
                            reads=[("cg", i), ("cgh", i), ("acc", i), "cst"], writes=[("acc", i)])

            cv_slot = {}

            def conv_act(i, k):
                w = cst[:, C_DWW + i * 31 + k:C_DWW + i * 31 + k + 1]
                cv = r_cv.next()
                cv_slot[(i, k)] = cv
                P.add("act", lambda e: e.activation(out=mT[:, cv, 0:128], in_=identb[:], func=AF.Copy, scale=w),
                      reads=["identb", "cst"], writes=[("mT", cv)])

            def conv_mm(i, k):
                cv = cv_slot[(i, k)]
                P.add("pe", lambda e: e.matmul(pfb(5), lhsT=mT[:, cv, 0:128], rhs=cgb[:, i - 1, k:k + 512],
                                               start=(k == 0), stop=(k == 30)),
                      reads=[("mT", cv), ("cg", i), ("cgh", i)], writes=[("pf", 5)])
                if k == 30:
                    P.add("act", lambda e: e.activation(out=acc[:, i, :], in_=pfb(5), func=AF.Identity,
                                                        bias=cst[:, C_DWB + i:C_DWB + i + 1], scale=1.0),
                          reads=[("pf", 5), "cst"], writes=[("acc", i)])

            def layernorm_silu():
                for i in range(4):
                    P.add("act", lambda e, i=i: e.activation(out=sq, in_=acc[:, i, :], func=AF.Square),
                          reads=[("acc", i)], writes=["sq"])
                    P.add("pe", lambda e, i=i: e.matmul(pfb(4), lhsT=ones32[:], rhs=acc[:, i, :],
                                                        start=(i == 0), stop=(i == 3)),
                          reads=[("acc", i), "ones32"], writes=[("pf", 4)])
                    P.add("pe", lambda e, i=i: e.matmul(pfb(5), lhsT=ones32[:], rhs=sq,
                                                        start=(i == 0), stop=(i == 3)),
                          reads=["sq", "ones32"], writes=[("pf", 5)])
                P.add("dve", lambda e: e.tensor_scalar(out=mean, in0=pfb(4), scalar1=1.0 / 512, scalar2=None,
                                                       op0=ALU.mult),
                      reads=[("pf", 4)], writes=["mean"])
                P.add("dve", lambda e: e.tensor_tensor(out=tmpv, in0=mean, in1=mean, op=ALU.mult),
                      reads=["mean"], writes=["tmpv"])
                P.add("dve", lambda e: e.scalar_tensor_tensor(out=tmpv, in0=pfb(5), scalar=1.0 / 512, in1=tmpv,
                                                              op0=ALU.mult, op1=ALU.subtract),
                      reads=[("pf", 5), "tmpv"], writes=["tmpv"])
                P.add("dve", lambda e: e.tensor_scalar(out=tmpv, in0=tmpv, scalar1=EPS, scalar2=None, op0=ALU.add),
                      reads=["tmpv"], writes=["tmpv"])
                P.add("act", lambda e: e.activation(out=rstdt, in_=tmpv, func=AF.Sqrt),
                      reads=["tmpv"], writes=["rstdt"])
                P.add("dve", lambda e: e.reciprocal(out=rstdt, in_=rstdt), reads=["rstdt"], writes=["rstdt"])
                for i in range(4):
                    P.add("dve", lambda e, i=i: e.tensor_tensor(out=acc[:, i, :], in0=acc[:, i, :], in1=mean,
                                                                op=ALU.subtract),
                          reads=[("acc", i), "mean"], writes=[("acc", i)])
                    P.add("dve", lambda e, i=i: e.tensor_tensor(out=acc[:, i, :], in0=acc[:, i, :], in1=rstdt,
                                                                op=ALU.mult),
                          reads=[("acc", i), "rstdt"], writes=[("acc", i)])
                    P.add("act", lambda e, i=i: e.activation(
                        out=so[:, i, :], in_=acc[:, i, :], func=AF.Silu,
                        bias=cst[:, C_LNB + i:C_LNB + i + 1], scale=cst[:, C_LNG + i:C_LNG + i + 1]),
                        reads=[("acc", i), "cst"], writes=[("so", i)])

            items = [(qt, hd) for qt in range(4) for hd in range(8)]
            n_it = len(items)
            st_ = {}
            qt_rs = {}

            def stage_qk(n):
                qt, hd = items[n]
                if hd == 0:
                    qt_rs[qt] = r_o.next()
                rs = qt_rs[qt]
                j = hd // 2
                r0 = 64 * (hd % 2)
                sp_ = r_s.next()
                b0, b1 = 2 * sp_, 2 * sp_ + 1
                kreads = [("qT", j), ("kTc", j), ("kTh", j)]
                P.add("pe", lambda e: e.matmul(
                    pfb(b0), lhsT=qT[r0:r0 + 64, j, qt * 128:(qt + 1) * 128],
                    rhs=kT[r0:r0 + 64, j, 128 * qt:128 * qt + 512], start=True, stop=True),
                    reads=kreads, writes=[("pf", b0)])
                P.add("pe", lambda e: e.matmul(
                    pfb(b1, 128), lhsT=qT[r0:r0 + 64, j, qt * 128:(qt + 1) * 128],
                    rhs=kT[r0:r0 + 64, j, 128 * qt + 512:128 * qt + 640], start=True, stop=True),
                    reads=kreads, writes=[("pf", b1)])
                ss = r_ssb.next()
                s = s_sb[ss]
                P.add("dve", lambda e: e.tensor_tensor(
                    out=s[:, 0:512], in0=pfb(b0), in1=biasT[:, hd, 0:512], op=ALU.add),
                    reads=[("pf", b0), "biasT"], writes=[("s", ss, 0)])
                P.add("dve", lambda e: e.tensor_tensor(
                    out=s[:, 512:640], in0=pfb(b1, 128), in1=biasT[:, hd, 512:640], op=ALU.add),
                    reads=[("pf", b1), "biasT"], writes=[("s", ss, 1)])
                if halo_mask:
                    ncol = 512 - 128 * qt
                    P.add("dve", lambda e: e.tensor_scalar(
                        out=s[:, 0:ncol], in0=s[:, 0:ncol], scalar1=hm[:, 0:1], scalar2=None, op0=ALU.add),
                        reads=[("s", ss, 0), "hm"], writes=[("s", ss, 0)])
                a = r_ast.next()
                P.add("dve", lambda e: e.tensor_reduce(out=ast[:, a:a + 1], in_=s, axis=AX.X,
                                                       op=ALU.max, negate=True),
                      reads=[("s", ss, 0), ("s", ss, 1)], writes=[("ast", a)])
                pi = r_p.next()
                p = p_sb[pi]
                P.add("act", lambda e: e.activation(
                    out=p, in_=s, func=AF.Exp, bias=ast[:, a:a + 1], scale=1.0,
                    accum_out=rsum[:, rs, hd:hd + 1]),
                    reads=[("s", ss, 0), ("s", ss, 1), ("ast", a)], writes=[("p", pi), ("rsum", rs, hd)])
                st_[n] = dict(pi=pi, rs=rs)

            def stage_t(n):
                d = st_[n]
                pi = d["pi"]
                p = p_sb[pi]
                tb = r_pb.next()
                for jb in range(5):
                    P.add("pe", lambda e, jb=jb: e.transpose(
                        out=pbb(tb, 128, jb * 128), in_=p[:, jb * 128:(jb + 1) * 128], identity=identb[:]),
                        reads=[("p", pi), "identb"], writes=[("pb", tb)])
                ti_ = r_PT.next()
                PT = PT_sb[ti_]
                P.add("act", lambda e: e.activation(out=PT, in_=pbb(tb, 640), func=AF.Copy),
                      reads=[("pb", tb)], writes=[("PT", ti_)])
                d["ti"] = ti_

            def stage_pv(n):
                qt, hd = items[n]
                d = st_[n]
                ti_ = d["ti"]
                rs = d["rs"]
                ob = 4
                PT = PT_sb[ti_]
                for jb in range(5):
                    P.add("pe", lambda e, jb=jb: e.matmul(
                        pfb(ob, 64, hd * 64), lhsT=PT[:, jb * 128:(jb + 1) * 128],
                        rhs=vt[:, qt + jb, hd * 64:(hd + 1) * 64], start=(jb == 0), stop=(jb == 4)),
                        reads=[("PT", ti_), ("v", qt + jb)], writes=[("pf", ob)])
                if hd == 7:
                    P.add("dve", lambda e: e.reciprocal(out=rinv[:, rs, :], in_=rsum[:, rs, :]),
                          reads=[("rsum", rs, h_) for h_ in range(8)], writes=[("rinv", rs)])
                    for h_ in range(8):
                        P.add("act", lambda e, h_=h_: e.activation(
                            out=o_sb[:, h_ * 64:(h_ + 1) * 64], in_=pfb(ob, 64, h_ * 64), func=AF.Copy,
                            scale=rinv[:, rs, h_:h_ + 1]),
                            reads=[("pf", ob), ("rinv", rs)], writes=[("o_sb", h_)])
                    return qt
                return None

            def finalize(qt):
                tb = r_pb.next()
                for fc in range(4):
                    P.add("pe", lambda e, fc=fc: e.transpose(
                        out=pbb(tb, 128, fc * 128), in_=o_sb[:, fc * 128:(fc + 1) * 128], identity=identb[:]),
                        reads=[("o_sb", 2 * fc), ("o_sb", 2 * fc + 1), "identb"], writes=[("pb", tb)])
                for fc in range(4):
                    P.add("act", lambda e, fc=fc: e.activation(
                        out=attnT[:, fc, qt * 128:(qt + 1) * 128], in_=pbb(tb, 128, fc * 128), func=AF.Copy),
                        reads=[("pb", tb)], writes=[("attnT", fc, qt)])

            pending = None
            cv_list = [(i, k) for i in (1, 2, 3) for k in range(31)]
            cv_prev = []
            for step in range(n_it + 3):
                if step < 31:
                    conv_tap(step)
                for it in cv_prev:
                    conv_mm(*it)
                cv_prev = []
                for _ in range(3):
                    if cv_list:
                        it = cv_list.pop(0)
                        conv_act(*it)
                        cv_prev.append(it)
                if step < n_it:
                    stage_qk(step)
                if 0 <= step - 1 < n_it:
                    stage_t(step - 1)
                if pending is not None:
                    finalize(pending)
                    pending = None
                if 0 <= step - 2 < n_it:
                    pending = stage_pv(step - 2)
            layernorm_silu()

            for dc in range(8):
                sa = stream.get()
                sbk = stream.get()
                proj(sa, 0)
                proj(sbk, 1)
                for fc in range(4):
                    P.add("pe", lambda e, fc=fc, dc=dc: e.matmul(
                        pfb(2), lhsT=waot[:, fc, dc * 128:(dc + 1) * 128], rhs=attnT[:, fc, :],
                        start=(fc == 0), stop=(fc == 3)),
                        reads=["wao"] + [("attnT", fc, q) for q in range(4)], writes=[("pf", 2)])
                for fc in range(4):
                    P.add("pe", lambda e, fc=fc, dc=dc: e.matmul(
                        pfb(3), lhsT=wcot[:, fc, dc * 128:(dc + 1) * 128], rhs=so[:, fc, :],
                        start=(fc == 0), stop=(fc == 3)),
                        reads=["wco", ("so", fc)], writes=[("pf", 3)])
                P.add("act", lambda e, dc=dc: e.activation(out=sA, in_=pfb(0), func=AF.Sigmoid,
                                                           bias=cst[:, C_GATEB + dc:C_GATEB + dc + 1], scale=1.0),
                      reads=[("pf", 0), "cst"], writes=["sA"])
                P.add("act", lambda e, dc=dc: e.activation(out=sB, in_=pfb(1), func=AF.Sigmoid,
                                                           bias=cst[:, C_GATEB + 8 + dc:C_GATEB + 9 + dc], scale=1.0),
                      reads=[("pf", 1), "cst"], writes=["sB"])
                P.add("dve", lambda e: e.tensor_tensor(out=sA, in0=sA, in1=pfb(2), op=ALU.mult),
                      reads=["sA", ("pf", 2)], writes=["sA"])
                P.add("dve", lambda e: e.tensor_tensor(out=sB, in0=sB, in1=pfb(3), op=ALU.mult),
                      reads=["sB", ("pf", 3)], writes=["sB"])
                P.add("dve", lambda e, dc=dc: e.tensor_tensor(out=mT[:, dc, :], in0=sA, in1=sB, op=ALU.add),
                      reads=["sA", "sB"], writes=[("mT", dc)])
                stream.release(2)
            for qt, hi in enumerate(htiles):
                fb = r_f.next()
                banks = (2 * fb, 2 * fb + 1)
                for half in range(2):
                    for kc in range(8):
                        P.add("pe", lambda e, kc=kc, half=half, qt=qt, banks=banks: e.matmul(
                            pfb(banks[half]), lhsT=mT[:, kc, qt * 128:(qt + 1) * 128],
                            rhs=woutt[:, kc, half * 512:(half + 1) * 512], start=(kc == 0), stop=(kc == 7)),
                            reads=["wout", ("mT", kc)], writes=[("pf", banks[half])])
                postnorm(hi, banks, 1.0, ptmp_m)
            roll(False)

        ffn(0, [0, 1, 2, 3], xh, 0, None, 0)
        barrier()
        mixer([0, 1, 2, 3], True, False, first=True)
        for g in range(ng):
            ffn(0, list(range(8)), xo, g * 1024, None, 0)
            barrier()
            mixer([0, 1, 2, 3], False, g == 0, first=True)
            mixer([4, 5, 6, 7], False, False)
            ffn(1, list(range(8)), None, 0, y_d, g * 1024)
        P.add("sp", None, reads=[("y", i) for i in range(8)])
        P.emit(block, sems, dsems)
    return nc


_NC_CACHE = {}


def _prep_shared(inp):
    f = lambda a: np.ascontiguousarray(np.asarray(a, dtype=np.float32))

    def wgu(wg, wu):
        g = f(wg)[0].reshape(8, 128, NCH, 128).transpose(2, 1, 0, 3)
        u = f(wu)[0].reshape(8, 128, NCH, 128).transpose(2, 1, 0, 3)
        return np.ascontiguousarray(np.stack([g, u], axis=2).reshape(NCH, 128, 2048))

    def wdl(wd):
        return np.ascontiguousarray(f(wd)[0].reshape(NCH, 128, 1024))

    def rows(w, nk):
        n = w.shape[1]
        return np.ascontiguousarray(w.reshape(nk, 128, n).transpose(1, 0, 2).reshape(128, nk * n))

    w_in = f(inp["w_in"])[0]
    winb = np.ascontiguousarray(w_in.reshape(8, 128, 36, 128).transpose(2, 1, 0, 3).reshape(36, 128, 1024))
    wv = rows(w_in[:, 1024:1536], 8)
    wao = rows(f(inp["w_attn_out"])[0], 4)
    wco = rows(f(inp["conv_w_out"])[0], 4)
    wout = rows(f(inp["w_out"])[0], 8)
    rel = f(inp["rel_table"])[0]
    r = np.arange(128)[:, None]
    j = np.arange(640)[None, :]
    idx = np.clip(r + 512 - j, -128, 128) + 128
    vis = ((j // 64) >= (r // 64)) & ((j // 64) <= (r // 64) + 8)
    bias = rel[:, idx]
    bias = np.where(vis[None], bias, np.float32(-1e30)).astype(np.float32)
    biasT = np.ascontiguousarray(bias.transpose(1, 0, 2).reshape(128, 5120))

    def cols(v, n):
        return f(v).reshape(n, 128).T

    cst = np.zeros((128, NCST), np.float32)
    cst[:, C_GPRE1:C_GPRE1 + 8] = cols(inp["ffn1_norm_pre"][0], 8)
    cst[:, C_GPREM:C_GPREM + 8] = cols(inp["mix_norm_pre"][0], 8)
    cst[:, C_GPRE2:C_GPRE2 + 8] = cols(inp["ffn2_norm_pre"][0], 8)
    cst[:, C_GATEB:C_GATEB + 16] = cols(inp["gate_bias"][0], 16)
    glu = f(inp["conv_glu_bias"])[0]
    cst[:, C_GLUA:C_GLUA + 4] = cols(glu[:512], 4)
    cst[:, C_GLUB:C_GLUB + 4] = cols(glu[512:], 4)
    dw = f(inp["conv_dw_w"])[0][:, 0, :]
    cst[:, C_DWW:C_DWW + 124] = dw.reshape(31, 4, 128).transpose(2, 1, 0).reshape(128, 124)
    cst[:, C_DWB:C_DWB + 4] = cols(inp["conv_dw_b"][0], 4)
    cst[:, C_LNG:C_LNG + 4] = cols(inp["conv_ln_g"][0], 4)
    cst[:, C_LNB:C_LNB + 4] = cols(inp["conv_ln_b"][0], 4)
    gpost = np.stack([np.broadcast_to(f(inp[k])[0][None, :], (128, 1024))
                      for k in ("ffn1_norm_post", "mix_norm_post", "ffn2_norm_post")])
    return {
        "wgu1": wgu(inp["ffn1_w_gate"], inp["ffn1_w_up"]),
        "wgu2": wgu(inp["ffn2_w_gate"], inp["ffn2_w_up"]),
        "wd1": wdl(inp["ffn1_w_down"]),
        "wd2": wdl(inp["ffn2_w_down"]),
        "winb": winb, "wv": wv, "wao": wao, "wco": wco, "wout": wout,
        "biasT": biasT, "cst": cst, "gpost": np.ascontiguousarray(gpost),
        "ident": np.eye(128, dtype=np.float32),
    }


def kernel(**inputs):
    x = np.asarray(inputs["x"], dtype=np.float32)
    shared = _prep_shared(inputs)
    if "nc" not in _NC_CACHE:
        _NC_CACHE["nc"] = build_nc()
    nc = _NC_CACHE["nc"]
    in_maps = []
    for core in range(NCORES):
        b, half = core // 2, core % 2
        m = dict(shared)
        m["xo"] = np.ascontiguousarray(x[b, half * TOK:(half + 1) * TOK])
        hmv = np.zeros((128, 2), np.float32)
        if half == 0:
            m["xh"] = np.zeros((HALO, D), np.float32)
            hmv[:, 0] = -1e30
            hmv[:, 1] = 0.0
        else:
            m["xh"] = np.ascontiguousarray(x[b, TOK - HALO:TOK])
            hmv[:, 0] = 0.0
            hmv[:, 1] = 1.0
        m["hm"] = hmv
        in_maps.append(m)
    res = run_bass_kernel_spmd(nc, in_maps, core_ids=list(range(NCORES)))
    out = np.empty((4, 2 * TOK, D), np.float32)
    for core in range(NCORES):
        b, half = core // 2, core % 2
        out[b, half * TOK:(half + 1) * TOK] = res.results[core]["y"]
    return out
```

```python
import numpy as np
from contextlib import ExitStack
import concourse.bass as bass
import concourse.mybir as mybir
from concourse.bass_utils import run_bass_kernel_spmd

F32 = mybir.dt.float32
BF16 = mybir.dt.bfloat16
ALU = mybir.AluOpType
AF = mybir.ActivationFunctionType
AX = mybir.AxisListType

ENG = ("pe", "act", "dve", "pool", "sp")
EPS = 1e-6
D = 1024
DFF = 2816
NCH = 22
NCORES = 8
TOK = 4096
HALO = 512

C_GPRE1, C_GPREM, C_GPRE2 = 0, 8, 16
C_GATEB = 24
C_GLUA, C_GLUB = 40, 44
C_DWW = 48
C_DWB = 48 + 124
C_LNG = C_DWB + 4
C_LNB = C_LNG + 4
NCST = C_LNB + 4
MAX_OPS = None
EVAC_ACT = (0,)


class Op:
    __slots__ = ("eng", "fn", "dma", "pos", "flag", "waits", "dma_target", "done_clock")


class Prog:
    def __init__(self):
        self.ops = {e: [] for e in ENG}
        self.comp = {e: [] for e in ENG}
        self.last_w = {}
        self.readers = {}
        self.dma_count = {}
        self.eng_clock = {e: {} for e in ENG}
        self.phase_key = "phase"

    def add(self, eng, fn, reads=(), writes=(), dma=None):
        self.n_added = getattr(self, "n_added", 0) + 1
        if MAX_OPS is not None and self.n_added > MAX_OPS:
            return None
        op = Op()
        op.eng = eng
        op.fn = fn
        op.dma = dma
        op.flag = False
        reads = list(reads)
        reads.append(self.phase_key)
        deps = []
        lw = self.last_w
        rd = self.readers
        for k in reads:
            d = lw.get(k)
            if d is not None:
                deps.append(d)
            if isinstance(k, tuple) and (k[0] == "pf" or k[0] == "pb"):
                r = rd.get(k)
                if r:
                    deps.extend(x for x in r if x.eng != eng)
        for k in writes:
            d = lw.get(k)
            if d is not None:
                deps.append(d)
            r = rd.get(k)
            if r:
                deps.extend(r)
        clock = self.eng_clock[eng]
        waits = {}
        pe_self = (eng == "pe" and dma is None)
        for d in deps:
            if d.dma is not None:
                dim = ("dma", d.dma)
                val = d.dma_target
            else:
                if pe_self and d.eng == "pe":
                    continue
                dim = d.eng
                val = d.pos
            if clock.get(dim, 0) >= val:
                continue
            if waits.get(dim, 0) < val:
                waits[dim] = val
        for d in deps:
            if d.dma is None and pe_self and d.eng == "pe":
                continue
            for dim, val in d.done_clock.items():
                if clock.get(dim, 0) < val:
                    clock[dim] = val
        for dim, val in waits.items():
            if clock.get(dim, 0) < val:
                clock[dim] = val
            if not isinstance(dim, tuple):
                self.comp[dim][val - 1].flag = True
        op.waits = waits
        if fn is None:
            op.pos = None
            op.done_clock = None
            self.ops[eng].append(op)
            return op
        dc = dict(clock)
        if dma is not None:
            c = self.dma_count.get(dma, 0) + 1
            self.dma_count[dma] = c
            op.dma_target = c
            op.pos = None
            dc[("dma", dma)] = c
        else:
            self.comp[eng].append(op)
            op.pos = len(self.comp[eng])
            dc[eng] = op.pos
        op.done_clock = dc
        self.ops[eng].append(op)
        for k in reads:
            rd.setdefault(k, []).append(op)
        for k in writes:
            lw[k] = op
            rd[k] = []
        return op

    def emit(self, block, eng_sems, dma_sems):
        counts = {}
        for e in ENG:
            c = 0
            arr = []
            for op in self.comp[e]:
                if op.flag:
                    c += 1
                arr.append(c)
            counts[e] = arr

        def run(e, engine):
            for op in self.ops[e]:
                for dim, val in op.waits.items():
                    if isinstance(dim, tuple):
                        engine.wait_ge(dma_sems[dim[1]], 16 * val)
                    else:
                        engine.wait_ge(eng_sems[dim], counts[dim][val - 1])
                if op.fn is None:
                    continue
                ins = op.fn(engine)
                if op.dma is not None:
                    ins.then_inc(dma_sems[op.dma], 16)
                elif op.flag:
                    ins.then_inc(eng_sems[e], 1)

        block.tensor(lambda eng: run("pe", eng))
        block.scalar(lambda eng: run("act", eng))
        block.vector(lambda eng: run("dve", eng))
        block.gpsimd(lambda eng: run("pool", eng))
        block.sync(lambda eng: run("sp", eng))


class Rot:
    def __init__(self, n):
        self.n = n
        self.i = 0

    def next(self):
        v = self.i
        self.i = (self.i + 1) % self.n
        return v


def build_nc(ng=4):
    nc = bass.Bass("TRN2", target_bir_lowering=False)

    def din(name, shape):
        return nc.dram_tensor(name, shape, F32, kind="ExternalInput").ap()

    xo = din("xo", [TOK, D])
    xh = din("xh", [HALO, D])
    hm_d = din("hm", [128, 2])
    wgu_d = [din("wgu1", [NCH, 128, 2048]), din("wgu2", [NCH, 128, 2048])]
    wd_d = [din("wd1", [NCH, 128, 1024]), din("wd2", [NCH, 128, 1024])]
    winb_d = din("winb", [36, 128, 1024])
    wv_d = din("wv", [128, 4096])
    wao_d = din("wao", [128, 4096])
    wco_d = din("wco", [128, 4096])
    wout_d = din("wout", [128, 8192])
    bias_d = din("biasT", [128, 5120])
    cst_d = din("cst", [128, NCST])
    gpost_d = din("gpost", [3, 128, 1024])
    ident_d = din("ident", [128, 128])
    y_d = nc.dram_tensor("y", [TOK, D], F32, kind="ExternalOutput").ap()

    P = Prog()
    global LASTP
    LASTP = P
    es = ExitStack()
    with es:
        def sb(name, shape, dt):
            return es.enter_context(nc.sbuf_tensor(name, shape, dt))

        h = sb("h", [128, 8, 1024], F32)
        kT = sb("kT", [128, 4, 1024], BF16)
        vt = sb("vt", [128, 8, 512], BF16)
        gp = sb("gp", [128, 1024], F32)
        junk = sb("junk", [128, 2, 1024], BF16)
        mhalf = sb("mhalf", [128, 512], F32)
        st = sb("st", [128, 64], F32)
        ast = sb("ast", [128, 16], F32)
        rsum = sb("rsum", [128, 2, 8], F32)
        rinv = sb("rinv", [128, 2, 8], F32)
        cst = sb("cst_sb", [128, NCST], F32)
        identf = sb("identf", [128, 128], F32)
        identb = sb("identb", [128, 128], BF16)
        ones32 = sb("ones32", [128, 128], F32)
        hm = sb("hm_sb", [128, 2], F32)
        bscr = sb("bscr", [128, 2], F32)
        cgh_p = sb("cgh_p", [128, 4, 30], F32)
        ARENA_WORDS = 36800
        arena = sb("arena", [128, ARENA_WORDS], F32)
        pf = es.enter_context(nc.psum_tensor("pf", [128, 3072], F32))
        pb = es.enter_context(nc.psum_tensor("pb", [128, 2048], BF16))

        class Arena:
            def __init__(self):
                self.off = 0

            def take(self, free_shape, dt):
                nel = int(np.prod(free_shape))
                words = nel if dt == F32 else (nel + 1) // 2
                assert self.off + words <= ARENA_WORDS, (self.off, words)
                ap = arena[:, self.off:self.off + words]
                if dt == BF16:
                    ap = ap.bitcast(BF16)
                self.off += words
                if len(free_shape) == 2:
                    ap = ap.rearrange("p (a b) -> p a b", a=free_shape[0])
                elif len(free_shape) == 3:
                    ap = ap.rearrange("p (a b c) -> p a b c", a=free_shape[0], b=free_shape[1])
                return ap

        A = Arena()
        xnT = A.take((8, 1024), BF16)
        hT = A.take((NCH, 1024), BF16)
        wdt = A.take((NCH, 1024), BF16)
        NSG = 4
        wgus = [A.take((2, 8, 128), BF16) for _ in range(NSG)]
        sg = [A.take((512,), F32) for _ in range(2)]
        xn_f = [A.take((1024,), BF16) for _ in range(2)]
        ptmp_f = [A.take((512,), F32) for _ in range(2)]
        M = Arena()
        uT = M.take((8, 512), BF16)
        qT = M.take((4, 512), BF16)
        biasT = M.take((8, 640), F32)
        cgb = M.take((4, 544), BF16)
        acc = M.take((4, 512), F32)
        sq = M.take((512,), F32)
        mean = M.take((512,), F32)
        tmpv = M.take((512,), F32)
        rstdt = M.take((512,), F32)
        so = M.take((4, 512), BF16)
        attnT = M.take((4, 512), BF16)
        mT = M.take((8, 512), BF16)
        s_sb = [M.take((640,), F32) for _ in range(2)]
        p_sb = [M.take((640,), BF16) for _ in range(2)]
        PT_sb = [M.take((640,), BF16) for _ in range(2)]
        o_sb = M.take((512,), BF16)
        sA = M.take((512,), F32)
        sB = M.take((512,), F32)
        NSW = 4
        wsl = [M.take((8, 128), BF16) for _ in range(NSW)]
        wvt = M.take((8, 512), BF16)
        waot = M.take((4, 1024), BF16)
        wcot = M.take((4, 1024), BF16)
        woutt = M.take((8, 1024), BF16)
        xn_m = [M.take((1024,), BF16) for _ in range(2)]
        ptmp_m = [M.take((512,), F32) for _ in range(2)]

        sems = {e: es.enter_context(nc.semaphore("s_" + e)) for e in ENG}
        dkeys = (["h%d" % i for i in range(8)] + ["o%d" % i for i in range(8)] +
                 ["wgu%d" % i for i in range(NSG)] + ["wd", "gp", "cst", "id", "hm", "bias",
                                                      "wv", "wao", "wco", "wout"] +
                 ["ws%d" % i for i in range(NSW)] + ["rollk", "rollv"])
        dsems = {k: es.enter_context(nc.semaphore("d_" + k)) for k in dkeys}
        block = es.enter_context(nc.Block())

        def pfb(b, n=512, o=0):
            return pf[:, b * 512 + o:b * 512 + o + n]

        def pbb(b, n, o=0):
            return pb[:, b * 1024 + o:b * 1024 + o + n]

        r_st = Rot(16)
        r_junk = Rot(2)
        r_pb = Rot(2)
        r_xn = Rot(2)
        r_pt = Rot(2)
        r_g = Rot(2)
        r_u = Rot(2)
        r_sg = Rot(2)
        r_f = Rot(3)
        r_proj = Rot(4)
        r_s = Rot(2)
        r_ssb = Rot(2)
        r_p = Rot(2)
        r_PT = Rot(2)
        r_ast = Rot(16)
        r_o = Rot(2)
        r_alt = Rot(2)
        barrier_n = [0]

        def barrier():
            barrier_n[0] += 1
            old = P.phase_key
            op_fn = lambda e: e.memset(bscr[:, 0:1], 0.0)
            P.phase_key = "nophase"
            P.add("dve", op_fn, reads=[], writes=[old])
            P.phase_key = old

        P.add("sp", lambda e: e.dma_start(out=cst[:], in_=cst_d), writes=["cst"], dma="cst")
        P.add("sp", lambda e: e.dma_start(out=identf[:], in_=ident_d), writes=["identf"], dma="id")
        P.add("sp", lambda e: e.dma_start(out=hm[:], in_=hm_d), writes=["hm"], dma="hm")
        P.add("dve", lambda e: e.tensor_copy(out=identb[:], in_=identf[:]), reads=["identf"], writes=["identb"])
        P.add("pool", lambda e: e.memset(mhalf[:], -0.5), writes=["mhalf"])
        P.add("pool", lambda e: e.memset(ones32[:], 1.0), writes=["ones32"])

        def rstd_from(srcs, reads):
            s = r_st.next()
            key = ("st", s)
            n = srcs[0].shape[-1]
            for idx, src in enumerate(srcs):
                jk = r_junk.next()
                P.add("act", lambda e, src=src, c=4 * s + idx, jk=jk: e.activation(
                    out=junk[:, jk, 0:n], in_=src, func=AF.Square, accum_out=st[:, c:c + 1]),
                    reads=reads, writes=[(key, idx), ("junk", jk)])
            if len(srcs) == 2:
                P.add("dve", lambda e, s=s: e.tensor_tensor(
                    out=st[:, 4 * s:4 * s + 1], in0=st[:, 4 * s:4 * s + 1], in1=st[:, 4 * s + 1:4 * s + 2],
                    op=ALU.add),
                    reads=[(key, 0), (key, 1)], writes=[(key, 0)])
            P.add("dve", lambda e, s=s: e.tensor_scalar(
                out=st[:, 4 * s + 2:4 * s + 3], in0=st[:, 4 * s:4 * s + 1], scalar1=1.0 / 1024,
                scalar2=EPS, op0=ALU.mult, op1=ALU.add),
                reads=[(key, 0), (key, 1)], writes=[(key, 2)])
            P.add("pool", lambda e, s=s: e.tensor_tensor(
                out=st[:, 4 * s + 3:4 * s + 4], in0=st[:, 4 * s + 2:4 * s + 3], in1=mhalf[:, 0:1], op=ALU.pow),
                reads=[(key, 2), "mhalf"], writes=[(key, 3)])
            return st[:, 4 * s + 3:4 * s + 4], (key, 3)

        def prenorm_stats(hi):
            return rstd_from([h[:, hi, :]], [("h", hi)])

        def prenorm_apply(hi, rr, xn_bufs, dst, dstkey, col0, g0):
            r, rkey = rr
            xs = r_xn.next()
            xn = xn_bufs[xs]
            P.add("dve", lambda e: e.tensor_scalar(out=xn, in0=h[:, hi, :], scalar1=r, scalar2=None,
                                                   op0=ALU.mult),
                  reads=[("h", hi), rkey], writes=[("xn", xs)])
            tb = r_pb.next()
            for c in range(8):
                P.add("pe", lambda e, c=c: e.transpose(out=pbb(tb, 128, c * 128), in_=xn[:, c * 128:(c + 1) * 128],
                                                       identity=identb[:]),
                      reads=[("xn", xs), "identb"], writes=[("pb", tb)])
            for c in range(8):
                o = dst[:, c, col0:col0 + 128]
                i_ = pbb(tb, 128, c * 128)
                g = cst[:, g0 + c:g0 + c + 1]
                if tb == 0:
                    P.add("act", lambda e, o=o, i_=i_, g=g: e.activation(out=o, in_=i_, func=AF.Copy, scale=g),
                          reads=[("pb", tb), "cst"], writes=[(dstkey, c)])
                else:
                    P.add("dve", lambda e, o=o, i_=i_, g=g: e.tensor_scalar(out=o, in0=i_, scalar1=g, scalar2=None,
                                                                         op0=ALU.mult),
                          reads=[("pb", tb), "cst"], writes=[(dstkey, c)])

        def prenorm_tile(hi, xn_bufs, dst, dstkey, col0, g0):
            prenorm_apply(hi, prenorm_stats(hi), xn_bufs, dst, dstkey, col0, g0)

        def postnorm(hi, banks, factor, ptmp):
            r, rkey = rstd_from([pfb(banks[0]), pfb(banks[1])], [("pf", banks[0]), ("pf", banks[1])])
            for half in range(2):
                k = r_pt.next()
                tmp = ptmp[k]
                b = banks[half]
                P.add("dve", lambda e, tmp=tmp, b=b, half=half: e.scalar_tensor_tensor(
                    out=tmp, in0=pfb(b), scalar=r, in1=gp[:, half * 512:(half + 1) * 512],
                    op0=ALU.mult, op1=ALU.mult),
                    reads=[("pf", b), rkey, "gp"], writes=[("ptmp", k)])
                P.add("dve", lambda e, tmp=tmp, half=half: e.scalar_tensor_tensor(
                    out=h[:, hi, half * 512:(half + 1) * 512], in0=tmp, scalar=float(factor),
                    in1=h[:, hi, half * 512:(half + 1) * 512], op0=ALU.mult, op1=ALU.add),
                    reads=[("ptmp", k), ("h", hi)], writes=[("h", hi)])

        def ffn(which, htiles, xsrc, xoff, outdst, ooff):
            nt = len(htiles)
            nsub = nt // 4
            g0 = C_GPRE1 if which == 0 else C_GPRE2
            barrier()
            P.add("sp", lambda e: e.dma_start(out=gp[:], in_=gpost_d[0 if which == 0 else 2]),
                  writes=["gp"], dma="gp")

            def load_chunk(c):
                slot = c % NSG
                P.add("pool", lambda e: e.dma_start(
                    out=wgus[slot].rearrange("p a b c -> p (a b c)"), in_=wgu_d[which][c]),
                    writes=[("wgu", slot)], dma="wgu%d" % slot)
                P.add("pool", lambda e: e.dma_start(out=wdt[:, c, :], in_=wd_d[which][c]),
                      writes=[("wd", c)], dma="wd")

            rrs = []
            for ti, hi in enumerate(htiles):
                if xsrc is not None:
                    P.add("sp", lambda e, ti=ti, hi=hi: e.dma_start(
                        out=h[:, hi, :], in_=xsrc[xoff + ti * 128:xoff + (ti + 1) * 128, :]),
                        writes=[("h", hi)], dma="h%d" % hi)
            for ti, hi in enumerate(htiles):
                rrs.append(prenorm_stats(hi))
            for c in range(NSG):
                load_chunk(c)
            for ti, hi in enumerate(htiles):
                prenorm_apply(hi, rrs[ti], xn_f, xnT, ("xnT", ti), ti * 128, g0)
            if nsub == 2:
                units = [(0, 0), (1, 0), (2, 0), (0, 1), (1, 1), (2, 1)] + \
                        [(c, sb_) for c in range(3, NCH) for sb_ in range(2)]
            else:
                units = [(c, 0) for c in range(NCH)]
            done_cnt = {}
            for (c, sub) in units:
                slot = c % NSG
                if True:
                    gb = r_g.next()
                    ub = 2 + r_u.next()
                    for kc in range(8):
                        P.add("pe", lambda e, kc=kc, gb=gb, sub=sub, slot=slot: e.matmul(
                            pfb(gb), lhsT=wgus[slot][:, 0, kc, :], rhs=xnT[:, kc, sub * 512:(sub + 1) * 512],
                            start=(kc == 0), stop=(kc == 7)),
                            reads=[("wgu", slot)] + [(("xnT", t), kc) for t in range(4 * sub, 4 * sub + 4)],
                            writes=[("pf", gb)])
                    for kc in range(8):
                        P.add("pe", lambda e, kc=kc, ub=ub, sub=sub, slot=slot: e.matmul(
                            pfb(ub), lhsT=wgus[slot][:, 1, kc, :], rhs=xnT[:, kc, sub * 512:(sub + 1) * 512],
                            start=(kc == 0), stop=(kc == 7)),
                            reads=[("wgu", slot)] + [(("xnT", t), kc) for t in range(4 * sub, 4 * sub + 4)],
                            writes=[("pf", ub)])
                    sgs = r_sg.next()
                    P.add("act", lambda e, gb=gb, sgs=sgs: e.activation(out=sg[sgs], in_=pfb(gb), func=AF.Silu),
                          reads=[("pf", gb)], writes=[("sg", sgs)])
                    P.add("dve", lambda e, ub=ub, sgs=sgs, sub=sub, c=c: e.tensor_tensor(
                        out=hT[:, c, sub * 512:(sub + 1) * 512], in0=sg[sgs], in1=pfb(ub), op=ALU.mult),
                        reads=[("sg", sgs), ("pf", ub)], writes=[("hT", c, sub)])
                done_cnt[c] = done_cnt.get(c, 0) + 1
                if done_cnt[c] == nsub and c + NSG < NCH:
                    load_chunk(c + NSG)
            P.add("pe", None, reads=[("wd", c) for c in range(NCH)])
            for ti, hi in enumerate(htiles):
                sub = ti // 4
                fb = r_f.next()
                banks = (2 * fb, 2 * fb + 1)
                for half in range(2):
                    for c in range(NCH):
                        P.add("pe", lambda e, c=c, half=half, ti=ti, banks=banks: e.matmul(
                            pfb(banks[half]), lhsT=hT[:, c, ti * 128:(ti + 1) * 128],
                            rhs=wdt[:, c, half * 512:(half + 1) * 512], start=(c == 0), stop=(c == NCH - 1)),
                            reads=[("hT", c, sub), ("wd", c)], writes=[("pf", banks[half])])
                postnorm(hi, banks, 0.5, ptmp_f)
                if outdst is not None:
                    P.add("sp", lambda e, ti=ti, hi=hi: e.dma_start(
                        out=outdst[ooff + ti * 128:ooff + (ti + 1) * 128, :], in_=h[:, hi, :]),
                        reads=[("h", hi)], writes=[("y", hi)], dma="o%d" % hi)

        class Stream:
            def __init__(self, blocks):
                self.blocks = blocks
                self.issued = 0
                self.cons = 0
                for _ in range(min(NSW, len(blocks))):
                    self._issue()

            def _issue(self):
                n = self.issued
                slot = n % NSW
                b = self.blocks[n]
                P.add("pool", lambda e: e.dma_start(out=wsl[slot].rearrange("p a b -> p (a b)"), in_=winb_d[b]),
                      writes=[("ws", slot)], dma="ws%d" % slot)
                self.issued += 1

            def get(self):
                slot = self.cons % NSW
                self.cons += 1
                return slot

            def release(self, count=1):
                for _ in range(count):
                    if self.issued < len(self.blocks):
                        self._issue()

        def mixer_load_weights(partial):
            P.add("pool", lambda e: e.dma_start(out=wvt.rearrange("p a b -> p (a b)"), in_=wv_d),
                  writes=["wv"], dma="wv")

        def mixer_load_late():
            P.add("sp", lambda e: e.dma_start(out=biasT.rearrange("p a b -> p (a b)"), in_=bias_d),
                  writes=["biasT"], dma="bias")
            P.add("sp", lambda e: e.dma_start(out=gp[:], in_=gpost_d[1]), writes=["gp"], dma="gp")
            P.add("pool", lambda e: e.dma_start(out=waot.rearrange("p a b -> p (a b)"), in_=wao_d),
                  writes=["wao"], dma="wao")
            P.add("pool", lambda e: e.dma_start(out=wcot.rearrange("p a b -> p (a b)"), in_=wco_d),
                  writes=["wco"], dma="wco")
            P.add("pool", lambda e: e.dma_start(out=woutt.rearrange("p a b -> p (a b)"), in_=wout_d),
                  writes=["wout"], dma="wout")

        def mixer(htiles, partial, halo_mask, first=False):
            blocks = [0, 1, 2, 3, 4, 5, 6, 7, 12, 16, 13, 17, 14, 18, 15, 19]
            if not partial:
                for dc in range(8):
                    blocks += [20 + dc, 28 + dc]
            rrs = [prenorm_stats(hi) for hi in htiles]
            stream = Stream(blocks)
            if first:
                mixer_load_weights(partial)
            for ti, hi in enumerate(htiles):
                prenorm_apply(hi, rrs[ti], xn_m, uT, ("uT", ti), ti * 128, C_GPREM)
            if first and not partial:
                mixer_load_late()
            uT_reads = [(("uT", t), kc) for t in range(4) for kc in range(8)]

            def proj(slot, bank):
                for kc in range(8):
                    P.add("pe", lambda e, kc=kc: e.matmul(pfb(bank), lhsT=wsl[slot][:, kc, :], rhs=uT[:, kc, :],
                                                          start=(kc == 0), stop=(kc == 7)),
                          reads=[("ws", slot)] + [(("uT", t), kc) for t in range(4)], writes=[("pf", bank)])

            for j in range(4):
                slot = stream.get()
                bank = r_proj.next()
                proj(slot, bank)
                P.add("act", lambda e, j=j, bank=bank: e.activation(out=qT[:, j, :], in_=pfb(bank), func=AF.Copy,
                                                                    scale=0.125),
                      reads=[("pf", bank)], writes=[("qT", j)])
                stream.release()
            for j in range(4):
                slot = stream.get()
                bank = r_proj.next()
                proj(slot, bank)
                P.add("dve", lambda e, j=j, bank=bank: e.tensor_copy(out=kT[:, j, 512:1024], in_=pfb(bank)),
                      reads=[("pf", bank)], writes=[("kTc", j)])
                stream.release()
            for qt in range(4):
                bank = r_proj.next()
                for kc in range(8):
                    P.add("pe", lambda e, kc=kc, qt=qt, bank=bank: e.matmul(
                        pfb(bank), lhsT=uT[:, kc, qt * 128:(qt + 1) * 128], rhs=wvt[:, kc, :],
                        start=(kc == 0), stop=(kc == 7)),
                        reads=["wv", (("uT", qt), kc)], writes=[("pf", bank)])
                P.add("act", lambda e, qt=qt, bank=bank: e.activation(out=vt[:, 4 + qt, :], in_=pfb(bank),
                                                                      func=AF.Copy),
                      reads=[("pf", bank)], writes=[("v", 4 + qt)])
            for i in range(4):
                sa = stream.get()
                sbk = stream.get()
                ba = r_proj.next()
                bb = r_proj.next()
                proj(sa, ba)
                proj(sbk, bb)
                P.add("act", lambda e, i=i, bb=bb: e.activation(out=sB, in_=pfb(bb), func=AF.Sigmoid,
                                                                bias=cst[:, C_GLUB + i:C_GLUB + i + 1], scale=1.0),
                      reads=[("pf", bb), "cst"], writes=["sB"])
                cgo = cgb[:, i, 30:542]
                P.add("dve", lambda e, i=i, ba=ba, cgo=cgo: e.scalar_tensor_tensor(
                    out=cgo, in0=pfb(ba), scalar=cst[:, C_GLUA + i:C_GLUA + i + 1], in1=sB,
                    op0=ALU.add, op1=ALU.mult),
                    reads=[("pf", ba), "sB", "cst"], writes=[("cg", i)])
                stream.release(2)

            def roll(first):
                P.add("sp", lambda e: e.dma_start(out=kT[:, :, 0:512], in_=kT[:, :, 512:1024]),
                      reads=[("kTc", j) for j in range(4)], writes=[("kTh", j) for j in range(4)], dma="rollk")
                P.add("sp", lambda e: e.dma_start(out=vt[:, 0:4, :], in_=vt[:, 4:8, :]),
                      reads=[("v", 4 + q) for q in range(4)], writes=[("v", q) for q in range(4)], dma="rollv")
                if first:
                    P.add("dve", lambda e: e.tensor_scalar(out=cgh_p[:], in0=cgb[:, :, 512:542],
                                                           scalar1=hm[:, 1:2], scalar2=None, op0=ALU.mult),
                          reads=[("cg", i) for i in range(4)] + ["hm"], writes=["cgh_p"])
                else:
                    P.add("pool", lambda e: e.tensor_copy(out=cgh_p[:], in_=cgb[:, :, 512:542]),
                          reads=[("cg", i) for i in range(4)], writes=["cgh_p"])

            if partial:
                roll(True)
                return

            P.add("pool", lambda e: e.tensor_copy(out=cgb[:, :, 0:30], in_=cgh_p[:]),
                  reads=["cgh_p"], writes=[("cgh", i) for i in range(4)])

            sqb = sq.bitcast(BF16)
            r_cv = Rot(8)

            def conv_tap(k):
                for i in range(1):
                    w = cst[:, C_DWW + i * 31 + k:C_DWW + i * 31 + k + 1]
                    if k == 0:
                        P.add("act", lambda e, i=i, w=w: e.activation(
                            out=acc[:, i, :], in_=cg0[:, 0:512], func=AF.Identity, scale=w,
                            bias=cst[:, C_DWB + i:C_DWB + i + 1]),
                            reads=[("cg", i), ("cgh", i), "cst"], writes=[("acc", i)])
                    else:
                        P.add("dve", lambda e, i=i, w=w, k=k: e.scalar_tensor_tensor(
                            out=acc[:, i, :], in0=cg0[:, k:k + 512], scalar=w, in1=acc[:, i, :],
                            op0=ALU.mult, op1=ALU.add),
                            reads=[("cg", i), ("cgh", i), ("acc", i), "cst"], writes=[("acc", i)])

            cv_slot = {}

            def conv_act(i, k):
                w = cst[:, C_DWW + i * 31 + k:C_DWW + i * 31 + k + 1]
                cv = r_cv.next()
                cv_slot[(i, k)] = cv
                P.add("pool", lambda e: e.tensor_tensor(out=mT[:, cv, 0:128], in0=identb[:],
                                                        in1=w.to_broadcast([128, 128]), op=ALU.mult),
                      reads=["identb", "cst"], writes=[("mT", cv)])

            def conv_mm(i, k):
                cv = cv_slot[(i, k)]
                P.add("pe", lambda e: e.matmul(pfb(5), lhsT=mT[:, cv, 0:128], rhs=cgb[:, i, k:k + 512],
                                               start=(k == 0), stop=(k == 30)),
                      reads=[("mT", cv), ("cg", i), ("cgh", i)], writes=[("pf", 5)])
                if k == 30:
                    P.add("act", lambda e: e.activation(out=acc[:, i, :], in_=pfb(5), func=AF.Identity,
                                                        bias=cst[:, C_DWB + i:C_DWB + i + 1], scale=1.0),
                          reads=[("pf", 5), "cst"], writes=[("acc", i)])

            def layernorm_silu():
                for i in range(4):
                    P.add("act", lambda e, i=i: e.activation(out=sq, in_=acc[:, i, :], func=AF.Square),
                          reads=[("acc", i)], writes=["sq"])
                    P.add("pe", lambda e, i=i: e.matmul(pfb(4), lhsT=ones32[:], rhs=acc[:, i, :],
                                                        start=(i == 0), stop=(i == 3)),
                          reads=[("acc", i), "ones32"], writes=[("pf", 4)])
                    P.add("pe", lambda e, i=i: e.matmul(pfb(5), lhsT=ones32[:], rhs=sq,
                                                        start=(i == 0), stop=(i == 3)),
                          reads=["sq", "ones32"], writes=[("pf", 5)])
                P.add("dve", lambda e: e.tensor_scalar(out=mean, in0=pfb(4), scalar1=1.0 / 512, scalar2=None,
                                                       op0=ALU.mult),
                      reads=[("pf", 4)], writes=["mean"])
                P.add("dve", lambda e: e.tensor_tensor(out=tmpv, in0=mean, in1=mean, op=ALU.mult),
                      reads=["mean"], writes=["tmpv"])
                P.add("dve", lambda e: e.scalar_tensor_tensor(out=tmpv, in0=pfb(5), scalar=1.0 / 512, in1=tmpv,
                                                              op0=ALU.mult, op1=ALU.subtract),
                      reads=[("pf", 5), "tmpv"], writes=["tmpv"])
                P.add("dve", lambda e: e.tensor_scalar(out=tmpv, in0=tmpv, scalar1=EPS, scalar2=None, op0=ALU.add),
                      reads=["tmpv"], writes=["tmpv"])
                P.add("act", lambda e: e.activation(out=rstdt, in_=tmpv, func=AF.Sqrt),
                      reads=["tmpv"], writes=["rstdt"])
                P.add("dve", lambda e: e.reciprocal(out=rstdt, in_=rstdt), reads=["rstdt"], writes=["rstdt"])
                for i in range(4):
                    P.add("dve", lambda e, i=i: e.tensor_tensor(out=acc[:, i, :], in0=acc[:, i, :], in1=mean,
                                                                op=ALU.subtract),
                          reads=[("acc", i), "mean"], writes=[("acc", i)])
                    P.add("dve", lambda e, i=i: e.tensor_tensor(out=acc[:, i, :], in0=acc[:, i, :], in1=rstdt,
                                                                op=ALU.mult),
                          reads=[("acc", i), "rstdt"], writes=[("acc", i)])
                    P.add("act", lambda e, i=i: e.activation(
                        out=so[:, i, :], in_=acc[:, i, :], func=AF.Silu,
                        bias=cst[:, C_LNB + i:C_LNB + i + 1], scale=cst[:, C_LNG + i:C_LNG + i + 1]),
                        reads=[("acc", i), "cst"], writes=[("so", i)])

            items = [(qt, hd) for qt in range(4) for hd in range(8)]
            n_it = len(items)
            st_ = {}
            qt_rs = {}

            def stage_qk(n):
                qt, hd = items[n]
                if hd == 0:
                    qt_rs[qt] = r_o.next()
                rs = qt_rs[qt]
                j = hd // 2
                r0 = 64 * (hd % 2)
                sp_ = r_s.next()
                b0, b1 = 2 * sp_, 2 * sp_ + 1
                kreads = [("qT", j), ("kTc", j), ("kTh", j)]
                P.add("pe", lambda e: e.matmul(
                    pfb(b0), lhsT=qT[r0:r0 + 64, j, qt * 128:(qt + 1) * 128],
                    rhs=kT[r0:r0 + 64, j, 128 * qt:128 * qt + 512], start=True, stop=True),
                    reads=kreads, writes=[("pf", b0)])
                P.add("pe", lambda e: e.matmul(
                    pfb(b1, 128), lhsT=qT[r0:r0 + 64, j, qt * 128:(qt + 1) * 128],
                    rhs=kT[r0:r0 + 64, j, 128 * qt + 512:128 * qt + 640], start=True, stop=True),
                    reads=kreads, writes=[("pf", b1)])
                ss = r_ssb.next()
                s = s_sb[ss]
                P.add("dve", lambda e: e.tensor_tensor(
                    out=s[:, 0:512], in0=pfb(b0), in1=biasT[:, hd, 0:512], op=ALU.add),
                    reads=[("pf", b0), "biasT"], writes=[("s", ss, 0)])
                P.add("dve", lambda e: e.tensor_tensor(
                    out=s[:, 512:640], in0=pfb(b1, 128), in1=biasT[:, hd, 512:640], op=ALU.add),
                    reads=[("pf", b1), "biasT"], writes=[("s", ss, 1)])
                if halo_mask:
                    ncol = 512 - 128 * qt
                    P.add("dve", lambda e: e.tensor_scalar(
                        out=s[:, 0:ncol], in0=s[:, 0:ncol], scalar1=hm[:, 0:1], scalar2=None, op0=ALU.add),
                        reads=[("s", ss, 0), "hm"], writes=[("s", ss, 0)])
                a = r_ast.next()
                P.add("dve", lambda e: e.tensor_reduce(out=ast[:, a:a + 1], in_=s, axis=AX.X,
                                                       op=ALU.max, negate=True),
                      reads=[("s", ss, 0), ("s", ss, 1)], writes=[("ast", a)])
                pi = r_p.next()
                p = p_sb[pi]
                P.add("act", lambda e: e.activation(
                    out=p, in_=s, func=AF.Exp, bias=ast[:, a:a + 1], scale=1.0,
                    accum_out=rsum[:, rs, hd:hd + 1]),
                    reads=[("s", ss, 0), ("s", ss, 1), ("ast", a)], writes=[("p", pi), ("rsum", rs, hd)])
                st_[n] = dict(pi=pi, rs=rs)

            def stage_t(n):
                d = st_[n]
                pi = d["pi"]
                p = p_sb[pi]
                tb = r_pb.next()
                for jb in range(5):
                    P.add("pe", lambda e, jb=jb: e.transpose(
                        out=pbb(tb, 128, jb * 128), in_=p[:, jb * 128:(jb + 1) * 128], identity=identb[:]),
                        reads=[("p", pi), "identb"], writes=[("pb", tb)])
                ti_ = r_PT.next()
                PT = PT_sb[ti_]
                P.add("act", lambda e: e.activation(out=PT, in_=pbb(tb, 640), func=AF.Copy),
                      reads=[("pb", tb)], writes=[("PT", ti_)])
                d["ti"] = ti_

            def stage_pv(n):
                qt, hd = items[n]
                d = st_[n]
                ti_ = d["ti"]
                rs = d["rs"]
                ob = 4
                PT = PT_sb[ti_]
                for jb in range(5):
                    P.add("pe", lambda e, jb=jb: e.matmul(
                        pfb(ob, 64, hd * 64), lhsT=PT[:, jb * 128:(jb + 1) * 128],
                        rhs=vt[:, qt + jb, hd * 64:(hd + 1) * 64], start=(jb == 0), stop=(jb == 4)),
                        reads=[("PT", ti_), ("v", qt + jb)], writes=[("pf", ob)])
                if hd == 7:
                    P.add("dve", lambda e: e.reciprocal(out=rinv[:, rs, :], in_=rsum[:, rs, :]),
                          reads=[("rsum", rs, h_) for h_ in range(8)], writes=[("rinv", rs)])
                    for h_ in range(8):
                        P.add("act", lambda e, h_=h_: e.activation(
                            out=o_sb[:, h_ * 64:(h_ + 1) * 64], in_=pfb(ob, 64, h_ * 64), func=AF.Copy,
                            scale=rinv[:, rs, h_:h_ + 1]),
                            reads=[("pf", ob), ("rinv", rs)], writes=[("o_sb", h_)])
                    return qt
                return None

            def finalize(qt):
                tb = r_pb.next()
                for fc in range(4):
                    P.add("pe", lambda e, fc=fc: e.transpose(
                        out=pbb(tb, 128, fc * 128), in_=o_sb[:, fc * 128:(fc + 1) * 128], identity=identb[:]),
                        reads=[("o_sb", 2 * fc), ("o_sb", 2 * fc + 1), "identb"], writes=[("pb", tb)])
                for fc in range(4):
                    P.add("act", lambda e, fc=fc: e.activation(
                        out=attnT[:, fc, qt * 128:(qt + 1) * 128], in_=pbb(tb, 128, fc * 128), func=AF.Copy),
                        reads=[("pb", tb)], writes=[("attnT", fc, qt)])

            pending = None
            cv_list = [(i, k) for i in (0, 1, 2, 3) for k in range(31)]
            cv_prev = []
            for step in range(n_it + 3):
                for it in cv_prev:
                    conv_mm(*it)
                cv_prev = []
                for _ in range(4):
                    if cv_list:
                        it = cv_list.pop(0)
                        conv_act(*it)
                        cv_prev.append(it)
                if step < n_it:
                    stage_qk(step)
                if 0 <= step - 1 < n_it:
                    stage_t(step - 1)
                if pending is not None:
                    finalize(pending)
                    pending = None
                if 0 <= step - 2 < n_it:
                    pending = stage_pv(step - 2)
            layernorm_silu()

            for dc in range(8):
                sa = stream.get()
                sbk = stream.get()
                proj(sa, 0)
                proj(sbk, 1)
                for fc in range(4):
                    P.add("pe", lambda e, fc=fc, dc=dc: e.matmul(
                        pfb(2), lhsT=waot[:, fc, dc * 128:(dc + 1) * 128], rhs=attnT[:, fc, :],
                        start=(fc == 0), stop=(fc == 3)),
                        reads=["wao"] + [("attnT", fc, q) for q in range(4)], writes=[("pf", 2)])
                for fc in range(4):
                    P.add("pe", lambda e, fc=fc, dc=dc: e.matmul(
                        pfb(3), lhsT=wcot[:, fc, dc * 128:(dc + 1) * 128], rhs=so[:, fc, :],
                        start=(fc == 0), stop=(fc == 3)),
                        reads=["wco", ("so", fc)], writes=[("pf", 3)])
                P.add("act", lambda e, dc=dc: e.activation(out=sA, in_=pfb(0), func=AF.Sigmoid,
                                                           bias=cst[:, C_GATEB + dc:C_GATEB + dc + 1], scale=1.0),
                      reads=[("pf", 0), "cst"], writes=["sA"])
                P.add("act", lambda e, dc=dc: e.activation(out=sB, in_=pfb(1), func=AF.Sigmoid,
                                                           bias=cst[:, C_GATEB + 8 + dc:C_GATEB + 9 + dc], scale=1.0),
                      reads=[("pf", 1), "cst"], writes=["sB"])
                P.add("dve", lambda e: e.tensor_tensor(out=sA, in0=sA, in1=pfb(2), op=ALU.mult),
                      reads=["sA", ("pf", 2)], writes=["sA"])
                P.add("dve", lambda e: e.tensor_tensor(out=sB, in0=sB, in1=pfb(3), op=ALU.mult),
                      reads=["sB", ("pf", 3)], writes=["sB"])
                P.add("dve", lambda e, dc=dc: e.tensor_tensor(out=mT[:, dc, :], in0=sA, in1=sB, op=ALU.add),
                      reads=["sA", "sB"], writes=[("mT", dc)])
                stream.release(2)
            for qt, hi in enumerate(htiles):
                fb = r_f.next()
                banks = (2 * fb, 2 * fb + 1)
                for half in range(2):
                    for kc in range(8):
                        P.add("pe", lambda e, kc=kc, half=half, qt=qt, banks=banks: e.matmul(
                            pfb(banks[half]), lhsT=mT[:, kc, qt * 128:(qt + 1) * 128],
                            rhs=woutt[:, kc, half * 512:(half + 1) * 512], start=(kc == 0), stop=(kc == 7)),
                            reads=["wout", ("mT", kc)], writes=[("pf", banks[half])])
                postnorm(hi, banks, 1.0, ptmp_m)
            roll(False)

        ffn(0, [0, 1, 2, 3], xh, 0, None, 0)
        barrier()
        mixer([0, 1, 2, 3], True, False, first=True)
        for g in range(ng):
            ffn(0, list(range(8)), xo, g * 1024, None, 0)
            barrier()
            mixer([0, 1, 2, 3], False, g == 0, first=True)
            mixer([4, 5, 6, 7], False, False)
            ffn(1, list(range(8)), None, 0, y_d, g * 1024)
        P.add("sp", None, reads=[("y", i) for i in range(8)])
        P.emit(block, sems, dsems)
    return nc


_NC_CACHE = {}


def _prep_shared(inp):
    f = lambda a: np.ascontiguousarray(np.asarray(a, dtype=np.float32))

    def wgu(wg, wu):
        g = f(wg)[0].reshape(8, 128, NCH, 128).transpose(2, 1, 0, 3)
        u = f(wu)[0].reshape(8, 128, NCH, 128).transpose(2, 1, 0, 3)
        return np.ascontiguousarray(np.stack([g, u], axis=2).reshape(NCH, 128, 2048))

    def wdl(wd):
        return np.ascontiguousarray(f(wd)[0].reshape(NCH, 128, 1024))

    def rows(w, nk):
        n = w.shape[1]
        return np.ascontiguousarray(w.reshape(nk, 128, n).transpose(1, 0, 2).reshape(128, nk * n))

    w_in = f(inp["w_in"])[0]
    winb = np.ascontiguousarray(w_in.reshape(8, 128, 36, 128).transpose(2, 1, 0, 3).reshape(36, 128, 1024))
    wv = rows(w_in[:, 1024:1536], 8)
    wao = rows(f(inp["w_attn_out"])[0], 4)
    wco = rows(f(inp["conv_w_out"])[0], 4)
    wout = rows(f(inp["w_out"])[0], 8)
    rel = f(inp["rel_table"])[0]
    r = np.arange(128)[:, None]
    j = np.arange(640)[None, :]
    idx = np.clip(r + 512 - j, -128, 128) + 128
    vis = ((j // 64) >= (r // 64)) & ((j // 64) <= (r // 64) + 8)
    bias = rel[:, idx]
    bias = np.where(vis[None], bias, np.float32(-1e30)).astype(np.float32)
    biasT = np.ascontiguousarray(bias.transpose(1, 0, 2).reshape(128, 5120))

    def cols(v, n):
        return f(v).reshape(n, 128).T

    cst = np.zeros((128, NCST), np.float32)
    cst[:, C_GPRE1:C_GPRE1 + 8] = cols(inp["ffn1_norm_pre"][0], 8)
    cst[:, C_GPREM:C_GPREM + 8] = cols(inp["mix_norm_pre"][0], 8)
    cst[:, C_GPRE2:C_GPRE2 + 8] = cols(inp["ffn2_norm_pre"][0], 8)
    cst[:, C_GATEB:C_GATEB + 16] = cols(inp["gate_bias"][0], 16)
    glu = f(inp["conv_glu_bias"])[0]
    cst[:, C_GLUA:C_GLUA + 4] = cols(glu[:512], 4)
    cst[:, C_GLUB:C_GLUB + 4] = cols(glu[512:], 4)
    dw = f(inp["conv_dw_w"])[0][:, 0, :]
    cst[:, C_DWW:C_DWW + 124] = dw.reshape(31, 4, 128).transpose(2, 1, 0).reshape(128, 124)
    cst[:, C_DWB:C_DWB + 4] = cols(inp["conv_dw_b"][0], 4)
    cst[:, C_LNG:C_LNG + 4] = cols(inp["conv_ln_g"][0], 4)
    cst[:, C_LNB:C_LNB + 4] = cols(inp["conv_ln_b"][0], 4)
    gpost = np.stack([np.broadcast_to(f(inp[k])[0][None, :], (128, 1024))
                      for k in ("ffn1_norm_post", "mix_norm_post", "ffn2_norm_post")])
    return {
        "wgu1": wgu(inp["ffn1_w_gate"], inp["ffn1_w_up"]),
        "wgu2": wgu(inp["ffn2_w_gate"], inp["ffn2_w_up"]),
        "wd1": wdl(inp["ffn1_w_down"]),
        "wd2": wdl(inp["ffn2_w_down"]),
        "winb": winb, "wv": wv, "wao": wao, "wco": wco, "wout": wout,
        "biasT": biasT, "cst": cst, "gpost": np.ascontiguousarray(gpost),
        "ident": np.eye(128, dtype=np.float32),
    }


def kernel(**inputs):
    x = np.asarray(inputs["x"], dtype=np.float32)
    shared = _prep_shared(inputs)
    if "nc" not in _NC_CACHE:
        _NC_CACHE["nc"] = build_nc()
    nc = _NC_CACHE["nc"]
    in_maps = []
    for core in range(NCORES):
        b, half = core // 2, core % 2
        m = dict(shared)
        m["xo"] = np.ascontiguousarray(x[b, half * TOK:(half + 1) * TOK])
        hmv = np.zeros((128, 2), np.float32)
        if half == 0:
            m["xh"] = np.zeros((HALO, D), np.float32)
            hmv[:, 0] = -1e30
            hmv[:, 1] = 0.0
        else:
            m["xh"] = np.ascontiguousarray(x[b, TOK - HALO:TOK])
            hmv[:, 0] = 0.0
            hmv[:, 1] = 1.0
        m["hm"] = hmv
        in_maps.append(m)
    res = run_bass_kernel_spmd(nc, in_maps, core_ids=list(range(NCORES)))
    out = np.empty((4, 2 * TOK, D), np.float32)
    for core in range(NCORES):
        b, half = core // 2, core % 2
        out[b, half * TOK:(half + 1) * TOK] = res.results[core]["y"]
    return out
```

```python
import numpy as np
from contextlib import ExitStack
import concourse.bass as bass
import concourse.mybir as mybir
from concourse.bass_utils import run_bass_kernel_spmd

F32 = mybir.dt.float32
BF16 = mybir.dt.bfloat16
ALU = mybir.AluOpType
AF = mybir.ActivationFunctionType
AX = mybir.AxisListType

ENG = ("pe", "act", "dve", "pool", "sp")
EPS = 1e-6
D = 1024
DFF = 2816
NCH = 22
NCORES = 8
TOK = 4096
HALO = 512

C_GPRE1, C_GPREM, C_GPRE2 = 0, 8, 16
C_GATEB = 24
C_GLUA, C_GLUB = 40, 44
C_DWW = 48
C_DWB = 48 + 124
C_LNG = C_DWB + 4
C_LNB = C_LNG + 4
NCST = C_LNB + 4
MAX_OPS = None
EVAC_ACT = (0,)


class Op:
    __slots__ = ("eng", "fn", "dma", "pos", "flag", "waits", "dma_target", "done_clock")


class Prog:
    def __init__(self):
        self.ops = {e: [] for e in ENG}
        self.comp = {e: [] for e in ENG}
        self.last_w = {}
        self.readers = {}
        self.dma_count = {}
        self.eng_clock = {e: {} for e in ENG}
        self.phase_key = "phase"

    def add(self, eng, fn, reads=(), writes=(), dma=None):
        self.n_added = getattr(self, "n_added", 0) + 1
        if MAX_OPS is not None and self.n_added > MAX_OPS:
            return None
        op = Op()
        op.eng = eng
        op.fn = fn
        op.dma = dma
        op.flag = False
        reads = list(reads)
        reads.append(self.phase_key)
        deps = []
        lw = self.last_w
        rd = self.readers
        for k in reads:
            d = lw.get(k)
            if d is not None:
                deps.append(d)
            if isinstance(k, tuple) and (k[0] == "pf" or k[0] == "pb"):
                r = rd.get(k)
                if r:
                    deps.extend(x for x in r if x.eng != eng)
        for k in writes:
            d = lw.get(k)
            if d is not None:
                deps.append(d)
            r = rd.get(k)
            if r:
                deps.extend(r)
        clock = self.eng_clock[eng]
        waits = {}
        pe_self = (eng == "pe" and dma is None)
        for d in deps:
            if d.dma is not None:
                dim = ("dma", d.dma)
                val = d.dma_target
            else:
                if pe_self and d.eng == "pe":
                    continue
                dim = d.eng
                val = d.pos
            if clock.get(dim, 0) >= val:
                continue
            if waits.get(dim, 0) < val:
                waits[dim] = val
        for d in deps:
            if d.dma is None and pe_self and d.eng == "pe":
                continue
            for dim, val in d.done_clock.items():
                if clock.get(dim, 0) < val:
                    clock[dim] = val
        for dim, val in waits.items():
            if clock.get(dim, 0) < val:
                clock[dim] = val
            if not isinstance(dim, tuple):
                self.comp[dim][val - 1].flag = True
        op.waits = waits
        if fn is None:
            op.pos = None
            op.done_clock = None
            self.ops[eng].append(op)
            return op
        dc = dict(clock)
        if dma is not None:
            c = self.dma_count.get(dma, 0) + 1
            self.dma_count[dma] = c
            op.dma_target = c
            op.pos = None
            dc[("dma", dma)] = c
        else:
            self.comp[eng].append(op)
            op.pos = len(self.comp[eng])
            dc[eng] = op.pos
        op.done_clock = dc
        self.ops[eng].append(op)
        for k in reads:
            rd.setdefault(k, []).append(op)
        for k in writes:
            lw[k] = op
            rd[k] = []
        return op

    def emit(self, block, eng_sems, dma_sems):
        counts = {}
        for e in ENG:
            c = 0
            arr = []
            for op in self.comp[e]:
                if op.flag:
                    c += 1
                arr.append(c)
            counts[e] = arr

        def run(e, engine):
            for op in self.ops[e]:
                for dim, val in op.waits.items():
                    if isinstance(dim, tuple):
                        engine.wait_ge(dma_sems[dim[1]], 16 * val)
                    else:
                        engine.wait_ge(eng_sems[dim], counts[dim][val - 1])
                if op.fn is None:
                    continue
                ins = op.fn(engine)
                if op.dma is not None:
                    ins.then_inc(dma_sems[op.dma], 16)
                elif op.flag:
                    ins.then_inc(eng_sems[e], 1)

        block.tensor(lambda eng: run("pe", eng))
        block.scalar(lambda eng: run("act", eng))
        block.vector(lambda eng: run("dve", eng))
        block.gpsimd(lambda eng: run("pool", eng))
        block.sync(lambda eng: run("sp", eng))


class Rot:
    def __init__(self, n):
        self.n = n
        self.i = 0

    def next(self):
        v = self.i
        self.i = (self.i + 1) % self.n
        return v


def build_nc(ng=4):
    nc = bass.Bass("TRN2", target_bir_lowering=False)

    def din(name, shape):
        return nc.dram_tensor(name, shape, F32, kind="ExternalInput").ap()

    xo = din("xo", [TOK, D])
    xh = din("xh", [HALO, D])
    hm_d = din("hm", [128, 2])
    wgu_d = [din("wgu1", [NCH, 128, 2048]), din("wgu2", [NCH, 128, 2048])]
    wd_d = [din("wd1", [NCH, 128, 1024]), din("wd2", [NCH, 128, 1024])]
    winb_d = din("winb", [36, 128, 1024])
    wv_d = din("wv", [128, 4096])
    wao_d = din("wao", [128, 4096])
    wco_d = din("wco", [128, 4096])
    wout_d = din("wout", [128, 8192])
    bias_d = din("biasT", [128, 5120])
    cst_d = din("cst", [128, NCST])
    gpost_d = din("gpost", [3, 128, 1024])
    ident_d = din("ident", [128, 128])
    y_d = nc.dram_tensor("y", [TOK, D], F32, kind="ExternalOutput").ap()

    P = Prog()
    global LASTP
    LASTP = P
    es = ExitStack()
    with es:
        def sb(name, shape, dt):
            return es.enter_context(nc.sbuf_tensor(name, shape, dt))

        h = sb("h", [128, 8, 1024], F32)
        kT = sb("kT", [128, 4, 1024], BF16)
        vt = sb("vt", [128, 8, 512], BF16)
        gp = sb("gp", [128, 1024], F32)
        junk = sb("junk", [128, 2, 1024], BF16)
        mhalf = sb("mhalf", [128, 512], F32)
        st = sb("st", [128, 64], F32)
        ast = sb("ast", [128, 16], F32)
        rsum = sb("rsum", [128, 2, 8], F32)
        rinv = sb("rinv", [128, 2, 8], F32)
        cst = sb("cst_sb", [128, NCST], F32)
        identf = sb("identf", [128, 128], F32)
        identb = sb("identb", [128, 128], BF16)
        ones32 = sb("ones32", [128, 128], F32)
        hm = sb("hm_sb", [128, 2], F32)
        bscr = sb("bscr", [128, 2], F32)
        cgh_p = sb("cgh_p", [128, 4, 30], F32)
        ARENA_WORDS = 36800
        arena = sb("arena", [128, ARENA_WORDS], F32)
        pf = es.enter_context(nc.psum_tensor("pf", [128, 3072], F32))
        pb = es.enter_context(nc.psum_tensor("pb", [128, 2048], BF16))

        class Arena:
            def __init__(self):
                self.off = 0

            def take(self, free_shape, dt):
                nel = int(np.prod(free_shape))
                words = nel if dt == F32 else (nel + 1) // 2
                assert self.off + words <= ARENA_WORDS, (self.off, words)
                ap = arena[:, self.off:self.off + words]
                if dt == BF16:
                    ap = ap.bitcast(BF16)
                self.off += words
                if len(free_shape) == 2:
                    ap = ap.rearrange("p (a b) -> p a b", a=free_shape[0])
                elif len(free_shape) == 3:
                    ap = ap.rearrange("p (a b c) -> p a b c", a=free_shape[0], b=free_shape[1])
                return ap

        A = Arena()
        xnT = A.take((8, 1024), BF16)
        hT = A.take((NCH, 1024), BF16)
        wdt = A.take((NCH, 1024), BF16)
        NSG = 4
        wgus = [A.take((2, 8, 128), BF16) for _ in range(NSG)]
        sg = [A.take((512,), F32) for _ in range(2)]
        xn_f = [A.take((1024,), BF16) for _ in range(2)]
        ptmp_f = [A.take((512,), F32) for _ in range(2)]
        M = Arena()
        uT = M.take((8, 512), BF16)
        qT = M.take((4, 512), BF16)
        biasT = M.take((8, 640), F32)
        cg0 = M.take((542,), F32)
        cgb = M.take((3, 544), BF16)
        acc = M.take((4, 512), F32)
        sq = M.take((512,), F32)
        mean = M.take((512,), F32)
        tmpv = M.take((512,), F32)
        rstdt = M.take((512,), F32)
        so = M.take((4, 512), BF16)
        attnT = M.take((4, 512), BF16)
        mT = M.take((8, 512), BF16)
        s_sb = [M.take((640,), F32) for _ in range(2)]
        p_sb = [M.take((640,), BF16) for _ in range(2)]
        PT_sb = [M.take((640,), BF16) for _ in range(2)]
        o_sb = M.take((512,), BF16)
        sA = M.take((512,), F32)
        sB = M.take((512,), F32)
        NSW = 4
        wsl = [M.take((8, 128), BF16) for _ in range(NSW)]
        wvt = M.take((8, 512), BF16)
        waot = M.take((4, 1024), BF16)
        wcot = M.take((4, 1024), BF16)
        woutt = M.take((8, 1024), BF16)
        xn_m = [M.take((1024,), BF16) for _ in range(2)]
        ptmp_m = [M.take((512,), F32) for _ in range(2)]

        sems = {e: es.enter_context(nc.semaphore("s_" + e)) for e in ENG}
        dkeys = (["h%d" % i for i in range(8)] + ["o%d" % i for i in range(8)] +
                 ["wgu%d" % i for i in range(NSG)] + ["wd", "gp", "cst", "id", "hm", "bias",
                                                      "wv", "wao", "wco", "wout"] +
                 ["ws%d" % i for i in range(NSW)] + ["rollk", "rollv"])
        dsems = {k: es.enter_context(nc.semaphore("d_" + k)) for k in dkeys}
        block = es.enter_context(nc.Block())

        def pfb(b, n=512, o=0):
            return pf[:, b * 512 + o:b * 512 + o + n]

        def pbb(b, n, o=0):
            return pb[:, b * 1024 + o:b * 1024 + o + n]

        r_st = Rot(16)
        r_junk = Rot(2)
        r_pb = Rot(2)
        r_xn = Rot(2)
        r_pt = Rot(2)
        r_g = Rot(2)
        r_u = Rot(2)
        r_sg = Rot(2)
        r_f = Rot(3)
        r_proj = Rot(4)
        r_s = Rot(2)
        r_ssb = Rot(2)
        r_p = Rot(2)
        r_PT = Rot(2)
        r_ast = Rot(16)
        r_o = Rot(2)
        r_alt = Rot(2)
        barrier_n = [0]

        def barrier():
            barrier_n[0] += 1
            old = P.phase_key
            op_fn = lambda e: e.memset(bscr[:, 0:1], 0.0)
            P.phase_key = "nophase"
            P.add("dve", op_fn, reads=[], writes=[old])
            P.phase_key = old

        P.add("sp", lambda e: e.dma_start(out=cst[:], in_=cst_d), writes=["cst"], dma="cst")
        P.add("sp", lambda e: e.dma_start(out=identf[:], in_=ident_d), writes=["identf"], dma="id")
        P.add("sp", lambda e: e.dma_start(out=hm[:], in_=hm_d), writes=["hm"], dma="hm")
        P.add("dve", lambda e: e.tensor_copy(out=identb[:], in_=identf[:]), reads=["identf"], writes=["identb"])
        P.add("pool", lambda e: e.memset(mhalf[:], -0.5), writes=["mhalf"])
        P.add("pool", lambda e: e.memset(ones32[:], 1.0), writes=["ones32"])

        def rstd_from(srcs, reads):
            s = r_st.next()
            key = ("st", s)
            n = srcs[0].shape[-1]
            for idx, src in enumerate(srcs):
                jk = r_junk.next()
                P.add("act", lambda e, src=src, c=4 * s + idx, jk=jk: e.activation(
                    out=junk[:, jk, 0:n], in_=src, func=AF.Square, accum_out=st[:, c:c + 1]),
                    reads=reads, writes=[(key, idx), ("junk", jk)])
            if len(srcs) == 2:
                P.add("dve", lambda e, s=s: e.tensor_tensor(
                    out=st[:, 4 * s:4 * s + 1], in0=st[:, 4 * s:4 * s + 1], in1=st[:, 4 * s + 1:4 * s + 2],
                    op=ALU.add),
                    reads=[(key, 0), (key, 1)], writes=[(key, 0)])
            P.add("dve", lambda e, s=s: e.tensor_scalar(
                out=st[:, 4 * s + 2:4 * s + 3], in0=st[:, 4 * s:4 * s + 1], scalar1=1.0 / 1024,
                scalar2=EPS, op0=ALU.mult, op1=ALU.add),
                reads=[(key, 0), (key, 1)], writes=[(key, 2)])
            P.add("pool", lambda e, s=s: e.tensor_tensor(
                out=st[:, 4 * s + 3:4 * s + 4], in0=st[:, 4 * s + 2:4 * s + 3], in1=mhalf[:, 0:1], op=ALU.pow),
                reads=[(key, 2), "mhalf"], writes=[(key, 3)])
            return st[:, 4 * s + 3:4 * s + 4], (key, 3)

        def prenorm_stats(hi):
            return rstd_from([h[:, hi, :]], [("h", hi)])

        def prenorm_apply(hi, rr, xn_bufs, dst, dstkey, col0, g0):
            r, rkey = rr
            xs = r_xn.next()
            xn = xn_bufs[xs]
            P.add("dve", lambda e: e.tensor_scalar(out=xn, in0=h[:, hi, :], scalar1=r, scalar2=None,
                                                   op0=ALU.mult),
                  reads=[("h", hi), rkey], writes=[("xn", xs)])
            tb = r_pb.next()
            for c in range(8):
                P.add("pe", lambda e, c=c: e.transpose(out=pbb(tb, 128, c * 128), in_=xn[:, c * 128:(c + 1) * 128],
                                                       identity=identb[:]),
                      reads=[("xn", xs), "identb"], writes=[("pb", tb)])
            for c in range(8):
                o = dst[:, c, col0:col0 + 128]
                i_ = pbb(tb, 128, c * 128)
                g = cst[:, g0 + c:g0 + c + 1]
                if tb == 0:
                    P.add("act", lambda e, o=o, i_=i_, g=g: e.activation(out=o, in_=i_, func=AF.Copy, scale=g),
                          reads=[("pb", tb), "cst"], writes=[(dstkey, c)])
                else:
                    P.add("dve", lambda e, o=o, i_=i_, g=g: e.tensor_scalar(out=o, in0=i_, scalar1=g, scalar2=None,
                                                                         op0=ALU.mult),
                          reads=[("pb", tb), "cst"], writes=[(dstkey, c)])

        def prenorm_tile(hi, xn_bufs, dst, dstkey, col0, g0):
            prenorm_apply(hi, prenorm_stats(hi), xn_bufs, dst, dstkey, col0, g0)

        def postnorm(hi, banks, factor, ptmp):
            r, rkey = rstd_from([pfb(banks[0]), pfb(banks[1])], [("pf", banks[0]), ("pf", banks[1])])
            for half in range(2):
                k = r_pt.next()
                tmp = ptmp[k]
                b = banks[half]
                P.add("dve", lambda e, tmp=tmp, b=b, half=half: e.scalar_tensor_tensor(
                    out=tmp, in0=pfb(b), scalar=r, in1=gp[:, half * 512:(half + 1) * 512],
                    op0=ALU.mult, op1=ALU.mult),
                    reads=[("pf", b), rkey, "gp"], writes=[("ptmp", k)])
                P.add("dve", lambda e, tmp=tmp, half=half: e.scalar_tensor_tensor(
                    out=h[:, hi, half * 512:(half + 1) * 512], in0=tmp, scalar=float(factor),
                    in1=h[:, hi, half * 512:(half + 1) * 512], op0=ALU.mult, op1=ALU.add),
                    reads=[("ptmp", k), ("h", hi)], writes=[("h", hi)])

        def ffn(which, htiles, xsrc, xoff, outdst, ooff):
            nt = len(htiles)
            nsub = nt // 4
            g0 = C_GPRE1 if which == 0 else C_GPRE2
            barrier()
            P.add("sp", lambda e: e.dma_start(out=gp[:], in_=gpost_d[0 if which == 0 else 2]),
                  writes=["gp"], dma="gp")

            def load_chunk(c):
                slot = c % NSG
                P.add("pool", lambda e: e.dma_start(
                    out=wgus[slot].rearrange("p a b c -> p (a b c)"), in_=wgu_d[which][c]),
                    writes=[("wgu", slot)], dma="wgu%d" % slot)
                P.add("pool", lambda e: e.dma_start(out=wdt[:, c, :], in_=wd_d[which][c]),
                      writes=[("wd", c)], dma="wd")

            rrs = []
            for ti, hi in enumerate(htiles):
                if xsrc is not None:
                    P.add("sp", lambda e, ti=ti, hi=hi: e.dma_start(
                        out=h[:, hi, :], in_=xsrc[xoff + ti * 128:xoff + (ti + 1) * 128, :]),
                        writes=[("h", hi)], dma="h%d" % hi)
            for ti, hi in enumerate(htiles):
                rrs.append(prenorm_stats(hi))
            for c in range(NSG):
                load_chunk(c)
            for ti, hi in enumerate(htiles):
                prenorm_apply(hi, rrs[ti], xn_f, xnT, ("xnT", ti), ti * 128, g0)
            if nsub == 2:
                units = [(0, 0), (1, 0), (2, 0), (0, 1), (1, 1), (2, 1)] + \
                        [(c, sb_) for c in range(3, NCH) for sb_ in range(2)]
            else:
                units = [(c, 0) for c in range(NCH)]
            done_cnt = {}
            for (c, sub) in units:
                slot = c % NSG
                if True:
                    gb = r_g.next()
                    ub = 2 + r_u.next()
                    for kc in range(8):
                        P.add("pe", lambda e, kc=kc, gb=gb, sub=sub, slot=slot: e.matmul(
                            pfb(gb), lhsT=wgus[slot][:, 0, kc, :], rhs=xnT[:, kc, sub * 512:(sub + 1) * 512],
                            start=(kc == 0), stop=(kc == 7)),
                            reads=[("wgu", slot)] + [(("xnT", t), kc) for t in range(4 * sub, 4 * sub + 4)],
                            writes=[("pf", gb)])
                    for kc in range(8):
                        P.add("pe", lambda e, kc=kc, ub=ub, sub=sub, slot=slot: e.matmul(
                            pfb(ub), lhsT=wgus[slot][:, 1, kc, :], rhs=xnT[:, kc, sub * 512:(sub + 1) * 512],
                            start=(kc == 0), stop=(kc == 7)),
                            reads=[("wgu", slot)] + [(("xnT", t), kc) for t in range(4 * sub, 4 * sub + 4)],
                            writes=[("pf", ub)])
                    sgs = r_sg.next()
                    P.add("act", lambda e, gb=gb, sgs=sgs: e.activation(out=sg[sgs], in_=pfb(gb), func=AF.Silu),
                          reads=[("pf", gb)], writes=[("sg", sgs)])
                    P.add("dve", lambda e, ub=ub, sgs=sgs, sub=sub, c=c: e.tensor_tensor(
                        out=hT[:, c, sub * 512:(sub + 1) * 512], in0=sg[sgs], in1=pfb(ub), op=ALU.mult),
                        reads=[("sg", sgs), ("pf", ub)], writes=[("hT", c, sub)])
                done_cnt[c] = done_cnt.get(c, 0) + 1
                if done_cnt[c] == nsub and c + NSG < NCH:
                    load_chunk(c + NSG)
            P.add("pe", None, reads=[("wd", c) for c in range(NCH)])
            for ti, hi in enumerate(htiles):
                sub = ti // 4
                fb = r_f.next()
                banks = (2 * fb, 2 * fb + 1)
                for half in range(2):
                    for c in range(NCH):
                        P.add("pe", lambda e, c=c, half=half, ti=ti, banks=banks: e.matmul(
                            pfb(banks[half]), lhsT=hT[:, c, ti * 128:(ti + 1) * 128],
                            rhs=wdt[:, c, half * 512:(half + 1) * 512], start=(c == 0), stop=(c == NCH - 1)),
                            reads=[("hT", c, sub), ("wd", c)], writes=[("pf", banks[half])])
                postnorm(hi, banks, 0.5, ptmp_f)
                if outdst is not None:
                    P.add("sp", lambda e, ti=ti, hi=hi: e.dma_start(
                        out=outdst[ooff + ti * 128:ooff + (ti + 1) * 128, :], in_=h[:, hi, :]),
                        reads=[("h", hi)], writes=[("y", hi)], dma="o%d" % hi)

        class Stream:
            def __init__(self, blocks):
                self.blocks = blocks
                self.issued = 0
                self.cons = 0
                for _ in range(min(NSW, len(blocks))):
                    self._issue()

            def _issue(self):
                n = self.issued
                slot = n % NSW
                b = self.blocks[n]
                P.add("pool", lambda e: e.dma_start(out=wsl[slot].rearrange("p a b -> p (a b)"), in_=winb_d[b]),
                      writes=[("ws", slot)], dma="ws%d" % slot)
                self.issued += 1

            def get(self):
                slot = self.cons % NSW
                self.cons += 1
                return slot

            def release(self, count=1):
                for _ in range(count):
                    if self.issued < len(self.blocks):
                        self._issue()

        def mixer_load_weights(partial):
            P.add("pool", lambda e: e.dma_start(out=wvt.rearrange("p a b -> p (a b)"), in_=wv_d),
                  writes=["wv"], dma="wv")

        def mixer_load_bias():
            P.add("sp", lambda e: e.dma_start(out=biasT.rearrange("p a b -> p (a b)"), in_=bias_d),
                  writes=["biasT"], dma="bias")
            P.add("sp", lambda e: e.dma_start(out=gp[:], in_=gpost_d[1]), writes=["gp"], dma="gp")

        def mixer_load_late():
            P.add("pool", lambda e: e.dma_start(out=waot.rearrange("p a b -> p (a b)"), in_=wao_d),
                  writes=["wao"], dma="wao")
            P.add("pool", lambda e: e.dma_start(out=wcot.rearrange("p a b -> p (a b)"), in_=wco_d),
                  writes=["wco"], dma="wco")
            P.add("pool", lambda e: e.dma_start(out=woutt.rearrange("p a b -> p (a b)"), in_=wout_d),
                  writes=["wout"], dma="wout")

        def mixer(htiles, partial, halo_mask, first=False):
            blocks = [0, 1, 2, 3, 4, 5, 6, 7, 12, 16, 13, 17, 14, 18, 15, 19]
            if not partial:
                for dc in range(8):
                    blocks += [20 + dc, 28 + dc]
            rrs = [prenorm_stats(hi) for hi in htiles]
            stream = Stream(blocks)
            if first:
                mixer_load_weights(partial)
            for ti, hi in enumerate(htiles):
                prenorm_apply(hi, rrs[ti], xn_m, uT, ("uT", ti), ti * 128, C_GPREM)
            uT_reads = [(("uT", t), kc) for t in range(4) for kc in range(8)]

            def proj(slot, bank):
                for kc in range(8):
                    P.add("pe", lambda e, kc=kc: e.matmul(pfb(bank), lhsT=wsl[slot][:, kc, :], rhs=uT[:, kc, :],
                                                          start=(kc == 0), stop=(kc == 7)),
                          reads=[("ws", slot)] + [(("uT", t), kc) for t in range(4)], writes=[("pf", bank)])

            for j in range(4):
                slot = stream.get()
                bank = r_proj.next()
                proj(slot, bank)
                P.add("act", lambda e, j=j, bank=bank: e.activation(out=qT[:, j, :], in_=pfb(bank), func=AF.Copy,
                                                                    scale=0.125),
                      reads=[("pf", bank)], writes=[("qT", j)])
                stream.release()
            for j in range(4):
                slot = stream.get()
                bank = r_proj.next()
                proj(slot, bank)
                P.add("dve", lambda e, j=j, bank=bank: e.tensor_copy(out=kT[:, j, 512:1024], in_=pfb(bank)),
                      reads=[("pf", bank)], writes=[("kTc", j)])
                stream.release()
            if first and not partial:
                mixer_load_bias()
            for qt in range(4):
                bank = r_proj.next()
                for kc in range(8):
                    P.add("pe", lambda e, kc=kc, qt=qt, bank=bank: e.matmul(
                        pfb(bank), lhsT=uT[:, kc, qt * 128:(qt + 1) * 128], rhs=wvt[:, kc, :],
                        start=(kc == 0), stop=(kc == 7)),
                        reads=["wv", (("uT", qt), kc)], writes=[("pf", bank)])
                P.add("act", lambda e, qt=qt, bank=bank: e.activation(out=vt[:, 4 + qt, :], in_=pfb(bank),
                                                                      func=AF.Copy),
                      reads=[("pf", bank)], writes=[("v", 4 + qt)])
            for i in range(4):
                sa = stream.get()
                sbk = stream.get()
                ba = r_proj.next()
                bb = r_proj.next()
                proj(sa, ba)
                proj(sbk, bb)
                P.add("act", lambda e, i=i, bb=bb: e.activation(out=sB, in_=pfb(bb), func=AF.Sigmoid,
                                                                bias=cst[:, C_GLUB + i:C_GLUB + i + 1], scale=1.0),
                      reads=[("pf", bb), "cst"], writes=["sB"])
                cgo = cg0[:, 30:542] if i == 0 else cgb[:, i - 1, 30:542]
                P.add("dve", lambda e, i=i, ba=ba, cgo=cgo: e.scalar_tensor_tensor(
                    out=cgo, in0=pfb(ba), scalar=cst[:, C_GLUA + i:C_GLUA + i + 1], in1=sB,
                    op0=ALU.add, op1=ALU.mult),
                    reads=[("pf", ba), "sB", "cst"], writes=[("cg", i)])
                stream.release(2)

            if first and not partial:
                mixer_load_late()

            def roll(first):
                P.add("sp", lambda e: e.dma_start(out=kT[:, :, 0:512], in_=kT[:, :, 512:1024]),
                      reads=[("kTc", j) for j in range(4)], writes=[("kTh", j) for j in range(4)], dma="rollk")
                P.add("sp", lambda e: e.dma_start(out=vt[:, 0:4, :], in_=vt[:, 4:8, :]),
                      reads=[("v", 4 + q) for q in range(4)], writes=[("v", q) for q in range(4)], dma="rollv")
                if first:
                    P.add("dve", lambda e: e.tensor_scalar(out=cgh_p[:, 0, :], in0=cg0[:, 512:542],
                                                           scalar1=hm[:, 1:2], scalar2=None, op0=ALU.mult),
                          reads=[("cg", 0), "hm"], writes=[("cgh_p", 0)])
                    P.add("dve", lambda e: e.tensor_scalar(out=cgh_p[:, 1:4, :], in0=cgb[:, :, 512:542],
                                                           scalar1=hm[:, 1:2], scalar2=None, op0=ALU.mult),
                          reads=[("cg", i) for i in range(1, 4)] + ["hm"], writes=[("cgh_p", 1)])
                else:
                    P.add("pool", lambda e: e.tensor_copy(out=cgh_p[:, 0, :], in_=cg0[:, 512:542]),
                          reads=[("cg", 0)], writes=[("cgh_p", 0)])
                    P.add("pool", lambda e: e.tensor_copy(out=cgh_p[:, 1:4, :], in_=cgb[:, :, 512:542]),
                          reads=[("cg", i) for i in range(1, 4)], writes=[("cgh_p", 1)])

            if partial:
                roll(True)
                return

            P.add("pool", lambda e: e.tensor_copy(out=cg0[:, 0:30], in_=cgh_p[:, 0, :]),
                  reads=[("cgh_p", 0)], writes=[("cgh", 0)])
            P.add("pool", lambda e: e.tensor_copy(out=cgb[:, :, 0:30], in_=cgh_p[:, 1:4, :]),
                  reads=[("cgh_p", 1)], writes=[("cgh", i) for i in range(1, 4)])

            sqb = sq.bitcast(BF16)
            r_cv = Rot(8)

            def conv_tap(k):
                for i in range(1):
                    w = cst[:, C_DWW + i * 31 + k:C_DWW + i * 31 + k + 1]
                    if k == 0:
                        P.add("act", lambda e, i=i, w=w: e.activation(
                            out=acc[:, i, :], in_=cg0[:, 0:512], func=AF.Identity, scale=w,
                            bias=cst[:, C_DWB + i:C_DWB + i + 1]),
                            reads=[("cg", i), ("cgh", i), "cst"], writes=[("acc", i)])
                    else:
                        P.add("dve", lambda e, i=i, w=w, k=k: e.scalar_tensor_tensor(
                            out=acc[:, i, :], in0=cg0[:, k:k + 512], scalar=w, in1=acc[:, i, :],
                            op0=ALU.mult, op1=ALU.add),
                            reads=[("cg", i), ("cgh", i), ("acc", i), "cst"], writes=[("acc", i)])

            cv_slot = {}

            def conv_act(i, k):
                w = cst[:, C_DWW + i * 31 + k:C_DWW + i * 31 + k + 1]
                cv = r_cv.next()
                cv_slot[(i, k)] = cv
                P.add("pool", lambda e: e.tensor_tensor(out=mT[:, cv, 0:128], in0=identb[:],
                                                        in1=w.to_broadcast([128, 128]), op=ALU.mult),
                      reads=["identb", "cst"], writes=[("mT", cv)])

            def conv_mm(i, k):
                cv = cv_slot[(i, k)]
                P.add("pe", lambda e: e.matmul(pfb(5), lhsT=mT[:, cv, 0:128], rhs=cgb[:, i - 1, k:k + 512],
                                               start=(k == 0), stop=(k == 30)),
                      reads=[("mT", cv), ("cg", i), ("cgh", i)], writes=[("pf", 5)])
                if k == 30:
                    P.add("act", lambda e: e.activation(out=acc[:, i, :], in_=pfb(5), func=AF.Identity,
                                                        bias=cst[:, C_DWB + i:C_DWB + i + 1], scale=1.0),
                          reads=[("pf", 5), "cst"], writes=[("acc", i)])

            def layernorm_silu():
                for i in range(4):
                    P.add("act", lambda e, i=i: e.activation(out=sq, in_=acc[:, i, :], func=AF.Square),
                          reads=[("acc", i)], writes=["sq"])
                    P.add("pe", lambda e, i=i: e.matmul(pfb(4), lhsT=ones32[:], rhs=acc[:, i, :],
                                                        start=(i == 0), stop=(i == 3)),
                          reads=[("acc", i), "ones32"], writes=[("pf", 4)])
                    P.add("pe", lambda e, i=i: e.matmul(pfb(5), lhsT=ones32[:], rhs=sq,
                                                        start=(i == 0), stop=(i == 3)),
                          reads=["sq", "ones32"], writes=[("pf", 5)])
                P.add("dve", lambda e: e.tensor_scalar(out=mean, in0=pfb(4), scalar1=1.0 / 512, scalar2=None,
                                                       op0=ALU.mult),
                      reads=[("pf", 4)], writes=["mean"])
                P.add("dve", lambda e: e.tensor_tensor(out=tmpv, in0=mean, in1=mean, op=ALU.mult),
                      reads=["mean"], writes=["tmpv"])
                P.add("dve", lambda e: e.scalar_tensor_tensor(out=tmpv, in0=pfb(5), scalar=1.0 / 512, in1=tmpv,
                                                              op0=ALU.mult, op1=ALU.subtract),
                      reads=[("pf", 5), "tmpv"], writes=["tmpv"])
                P.add("dve", lambda e: e.tensor_scalar(out=tmpv, in0=tmpv, scalar1=EPS, scalar2=None, op0=ALU.add),
                      reads=["tmpv"], writes=["tmpv"])
                P.add("act", lambda e: e.activation(out=rstdt, in_=tmpv, func=AF.Sqrt),
                      reads=["tmpv"], writes=["rstdt"])
                P.add("dve", lambda e: e.reciprocal(out=rstdt, in_=rstdt), reads=["rstdt"], writes=["rstdt"])
                for i in range(4):
                    P.add("dve", lambda e, i=i: e.tensor_tensor(out=acc[:, i, :], in0=acc[:, i, :], in1=mean,
                                                                op=ALU.subtract),
                          reads=[("acc", i), "mean"], writes=[("acc", i)])
                    P.add("dve", lambda e, i=i: e.tensor_tensor(out=acc[:, i, :], in0=acc[:, i, :], in1=rstdt,
                                                                op=ALU.mult),
                          reads=[("acc", i), "rstdt"], writes=[("acc", i)])
                    P.add("act", lambda e, i=i: e.activation(
                        out=so[:, i, :], in_=acc[:, i, :], func=AF.Silu,
                        bias=cst[:, C_LNB + i:C_LNB + i + 1], scale=cst[:, C_LNG + i:C_LNG + i + 1]),
                        reads=[("acc", i), "cst"], writes=[("so", i)])

            items = [(qt, hd) for qt in range(4) for hd in range(8)]
            n_it = len(items)
            st_ = {}
            qt_rs = {}

            def stage_qk(n):
                qt, hd = items[n]
                if hd == 0:
                    qt_rs[qt] = r_o.next()
                rs = qt_rs[qt]
                j = hd // 2
                r0 = 64 * (hd % 2)
                sp_ = r_s.next()
                b0, b1 = 2 * sp_, 2 * sp_ + 1
                kreads = [("qT", j), ("kTc", j), ("kTh", j)]
                P.add("pe", lambda e: e.matmul(
                    pfb(b0), lhsT=qT[r0:r0 + 64, j, qt * 128:(qt + 1) * 128],
                    rhs=kT[r0:r0 + 64, j, 128 * qt:128 * qt + 512], start=True, stop=True),
                    reads=kreads, writes=[("pf", b0)])
                P.add("pe", lambda e: e.matmul(
                    pfb(b1, 128), lhsT=qT[r0:r0 + 64, j, qt * 128:(qt + 1) * 128],
                    rhs=kT[r0:r0 + 64, j, 128 * qt + 512:128 * qt + 640], start=True, stop=True),
                    reads=kreads, writes=[("pf", b1)])
                ss = r_ssb.next()
                s = s_sb[ss]
                P.add("dve", lambda e: e.tensor_tensor(
                    out=s[:, 0:512], in0=pfb(b0), in1=biasT[:, hd, 0:512], op=ALU.add),
                    reads=[("pf", b0), "biasT"], writes=[("s", ss, 0)])
                P.add("dve", lambda e: e.tensor_tensor(
                    out=s[:, 512:640], in0=pfb(b1, 128), in1=biasT[:, hd, 512:640], op=ALU.add),
                    reads=[("pf", b1), "biasT"], writes=[("s", ss, 1)])
                if halo_mask:
                    ncol = 512 - 128 * qt
                    P.add("dve", lambda e: e.tensor_scalar(
                        out=s[:, 0:ncol], in0=s[:, 0:ncol], scalar1=hm[:, 0:1], scalar2=None, op0=ALU.add),
                        reads=[("s", ss, 0), "hm"], writes=[("s", ss, 0)])
                a = r_ast.next()
                P.add("dve", lambda e: e.tensor_reduce(out=ast[:, a:a + 1], in_=s, axis=AX.X,
                                                       op=ALU.max, negate=True),
                      reads=[("s", ss, 0), ("s", ss, 1)], writes=[("ast", a)])
                pi = r_p.next()
                p = p_sb[pi]
                P.add("act", lambda e: e.activation(
                    out=p, in_=s, func=AF.Exp, bias=ast[:, a:a + 1], scale=1.0,
                    accum_out=rsum[:, rs, hd:hd + 1]),
                    reads=[("s", ss, 0), ("s", ss, 1), ("ast", a)], writes=[("p", pi), ("rsum", rs, hd)])
                st_[n] = dict(pi=pi, rs=rs)

            def stage_t(n):
                d = st_[n]
                pi = d["pi"]
                p = p_sb[pi]
                tb = r_pb.next()
                for jb in range(5):
                    P.add("pe", lambda e, jb=jb: e.transpose(
                        out=pbb(tb, 128, jb * 128), in_=p[:, jb * 128:(jb + 1) * 128], identity=identb[:]),
                        reads=[("p", pi), "identb"], writes=[("pb", tb)])
                ti_ = r_PT.next()
                PT = PT_sb[ti_]
                P.add("act", lambda e: e.activation(out=PT, in_=pbb(tb, 640), func=AF.Copy),
                      reads=[("pb", tb)], writes=[("PT", ti_)])
                d["ti"] = ti_

            def stage_pv(n):
                qt, hd = items[n]
                d = st_[n]
                ti_ = d["ti"]
                rs = d["rs"]
                ob = 4
                PT = PT_sb[ti_]
                for jb in range(5):
                    P.add("pe", lambda e, jb=jb: e.matmul(
                        pfb(ob, 64, hd * 64), lhsT=PT[:, jb * 128:(jb + 1) * 128],
                        rhs=vt[:, qt + jb, hd * 64:(hd + 1) * 64], start=(jb == 0), stop=(jb == 4)),
                        reads=[("PT", ti_), ("v", qt + jb)], writes=[("pf", ob)])
                if hd == 7:
                    P.add("dve", lambda e: e.reciprocal(out=rinv[:, rs, :], in_=rsum[:, rs, :]),
                          reads=[("rsum", rs, h_) for h_ in range(8)], writes=[("rinv", rs)])
                    for h_ in range(8):
                        P.add("act", lambda e, h_=h_: e.activation(
                            out=o_sb[:, h_ * 64:(h_ + 1) * 64], in_=pfb(ob, 64, h_ * 64), func=AF.Copy,
                            scale=rinv[:, rs, h_:h_ + 1]),
                            reads=[("pf", ob), ("rinv", rs)], writes=[("o_sb", h_)])
                    return qt
                return None

            def finalize(qt):
                tb = r_pb.next()
                for fc in range(4):
                    P.add("pe", lambda e, fc=fc: e.transpose(
                        out=pbb(tb, 128, fc * 128), in_=o_sb[:, fc * 128:(fc + 1) * 128], identity=identb[:]),
                        reads=[("o_sb", 2 * fc), ("o_sb", 2 * fc + 1), "identb"], writes=[("pb", tb)])
                for fc in range(4):
                    P.add("act", lambda e, fc=fc: e.activation(
                        out=attnT[:, fc, qt * 128:(qt + 1) * 128], in_=pbb(tb, 128, fc * 128), func=AF.Copy),
                        reads=[("pb", tb)], writes=[("attnT", fc, qt)])

            pending = None
            cv_list = [(i, k) for i in (1, 2, 3) for k in range(31)]
            cv_prev = []
            for step in range(n_it + 3):
                if step < 31:
                    conv_tap(step)
                for it in cv_prev:
                    conv_mm(*it)
                cv_prev = []
                for _ in range(3):
                    if cv_list:
                        it = cv_list.pop(0)
                        conv_act(*it)
                        cv_prev.append(it)
                if step < n_it:
                    stage_qk(step)
                if 0 <= step - 1 < n_it:
                    stage_t(step - 1)
                if pending is not None:
                    finalize(pending)
                    pending = None
                if 0 <= step - 2 < n_it:
                    pending = stage_pv(step - 2)
            layernorm_silu()

            for dc in range(8):
                sa = stream.get()
                sbk = stream.get()
                proj(sa, 0)
                proj(sbk, 1)
                for fc in range(4):
                    P.add("pe", lambda e, fc=fc, dc=dc: e.matmul(
                        pfb(2), lhsT=waot[:, fc, dc * 128:(dc + 1) * 128], rhs=attnT[:, fc, :],
                        start=(fc == 0), stop=(fc == 3)),
                        reads=["wao"] + [("attnT", fc, q) for q in range(4)], writes=[("pf", 2)])
                for fc in range(4):
                    P.add("pe", lambda e, fc=fc, dc=dc: e.matmul(
                        pfb(3), lhsT=wcot[:, fc, dc * 128:(dc + 1) * 128], rhs=so[:, fc, :],
                        start=(fc == 0), stop=(fc == 3)),
                        reads=["wco", ("so", fc)], writes=[("pf", 3)])
                P.add("act", lambda e, dc=dc: e.activation(out=sA, in_=pfb(0), func=AF.Sigmoid,
                                                           bias=cst[:, C_GATEB + dc:C_GATEB + dc + 1], scale=1.0),
                      reads=[("pf", 0), "cst"], writes=["sA"])
                P.add("act", lambda e, dc=dc: e.activation(out=sB, in_=pfb(1), func=AF.Sigmoid,
                                                           bias=cst[:, C_GATEB + 8 + dc:C_GATEB + 9 + dc], scale=1.0),
                      reads=[("pf", 1), "cst"], writes=["sB"])
                P.add("dve", lambda e: e.tensor_tensor(out=sA, in0=sA, in1=pfb(2), op=ALU.mult),
                      reads=["sA", ("pf", 2)], writes=["sA"])
                P.add("dve", lambda e: e.tensor_tensor(out=sB, in0=sB, in1=pfb(3), op=ALU.mult),
                      reads=["sB", ("pf", 3)], writes=["sB"])
                P.add("dve", lambda e, dc=dc: e.tensor_tensor(out=mT[:, dc, :], in0=sA, in1=sB, op=ALU.add),
                      reads=["sA", "sB"], writes=[("mT", dc)])
                stream.release(2)
            for qt, hi in enumerate(htiles):
                fb = r_f.next()
                banks = (2 * fb, 2 * fb + 1)
                for half in range(2):
                    for kc in range(8):
                        P.add("pe", lambda e, kc=kc, half=half, qt=qt, banks=banks: e.matmul(
                            pfb(banks[half]), lhsT=mT[:, kc, qt * 128:(qt + 1) * 128],
                            rhs=woutt[:, kc, half * 512:(half + 1) * 512], start=(kc == 0), stop=(kc == 7)),
                            reads=["wout", ("mT", kc)], writes=[("pf", banks[half])])
                postnorm(hi, banks, 1.0, ptmp_m)
            roll(False)

        ffn(0, [0, 1, 2, 3], xh, 0, None, 0)
        barrier()
        mixer([0, 1, 2, 3], True, False, first=True)
        for g in range(ng):
            ffn(0, list(range(8)), xo, g * 1024, None, 0)
            barrier()
            mixer([0, 1, 2, 3], False, g == 0, first=True)
            mixer([4, 5, 6, 7], False, False)
            ffn(1, list(range(8)), None, 0, y_d, g * 1024)
        P.add("sp", None, reads=[("y", i) for i in range(8)])
        P.emit(block, sems, dsems)
    return nc


_NC_CACHE = {}


def _prep_shared(inp):
    f = lambda a: np.ascontiguousarray(np.asarray(a, dtype=np.float32))

    def wgu(wg, wu):
        g = f(wg)[0].reshape(8, 128, NCH, 128).transpose(2, 1, 0, 3)
        u = f(wu)[0].reshape(8, 128, NCH, 128).transpose(2, 1, 0, 3)
        return np.ascontiguousarray(np.stack([g, u], axis=2).reshape(NCH, 128, 2048))

    def wdl(wd):
        return np.ascontiguousarray(f(wd)[0].reshape(NCH, 128, 1024))

    def rows(w, nk):
        n = w.shape[1]
        return np.ascontiguousarray(w.reshape(nk, 128, n).transpose(1, 0, 2).reshape(128, nk * n))

    w_in = f(inp["w_in"])[0]
    winb = np.ascontiguousarray(w_in.reshape(8, 128, 36, 128).transpose(2, 1, 0, 3).reshape(36, 128, 1024))
    wv = rows(w_in[:, 1024:1536], 8)
    wao = rows(f(inp["w_attn_out"])[0], 4)
    wco = rows(f(inp["conv_w_out"])[0], 4)
    wout = rows(f(inp["w_out"])[0], 8)
    rel = f(inp["rel_table"])[0]
    r = np.arange(128)[:, None]
    j = np.arange(640)[None, :]
    idx = np.clip(r + 512 - j, -128, 128) + 128
    vis = ((j // 64) >= (r // 64)) & ((j // 64) <= (r // 64) + 8)
    bias = rel[:, idx]
    bias = np.where(vis[None], bias, np.float32(-1e30)).astype(np.float32)
    biasT = np.ascontiguousarray(bias.transpose(1, 0, 2).reshape(128, 5120))

    def cols(v, n):
        return f(v).reshape(n, 128).T

    cst = np.zeros((128, NCST), np.float32)
    cst[:, C_GPRE1:C_GPRE1 + 8] = cols(inp["ffn1_norm_pre"][0], 8)
    cst[:, C_GPREM:C_GPREM + 8] = cols(inp["mix_norm_pre"][0], 8)
    cst[:, C_GPRE2:C_GPRE2 + 8] = cols(inp["ffn2_norm_pre"][0], 8)
    cst[:, C_GATEB:C_GATEB + 16] = cols(inp["gate_bias"][0], 16)
    glu = f(inp["conv_glu_bias"])[0]
    cst[:, C_GLUA:C_GLUA + 4] = cols(glu[:512], 4)
    cst[:, C_GLUB:C_GLUB + 4] = cols(glu[512:], 4)
    dw = f(inp["conv_dw_w"])[0][:, 0, :]
    cst[:, C_DWW:C_DWW + 124] = dw.reshape(31, 4, 128).transpose(2, 1, 0).reshape(128, 124)
    cst[:, C_DWB:C_DWB + 4] = cols(inp["conv_dw_b"][0], 4)
    cst[:, C_LNG:C_LNG + 4] = cols(inp["conv_ln_g"][0], 4)
    cst[:, C_LNB:C_LNB + 4] = cols(inp["conv_ln_b"][0], 4)
    gpost = np.stack([np.broadcast_to(f(inp[k])[0][None, :], (128, 1024))
                      for k in ("ffn1_norm_post", "mix_norm_post", "ffn2_norm_post")])
    return {
        "wgu1": wgu(inp["ffn1_w_gate"], inp["ffn1_w_up"]),
        "wgu2": wgu(inp["ffn2_w_gate"], inp["ffn2_w_up"]),
        "wd1": wdl(inp["ffn1_w_down"]),
        "wd2": wdl(inp["ffn2_w_down"]),
        "winb": winb, "wv": wv, "wao": wao, "wco": wco, "wout": wout,
        "biasT": biasT, "cst": cst, "gpost": np.ascontiguousarray(gpost),
        "ident": np.eye(128, dtype=np.float32),
    }


def kernel(**inputs):
    x = np.asarray(inputs["x"], dtype=np.float32)
    shared = _prep_shared(inputs)
    if "nc" not in _NC_CACHE:
        _NC_CACHE["nc"] = build_nc()
    nc = _NC_CACHE["nc"]
    in_maps = []
    for core in range(NCORES):
        b, half = core // 2, core % 2
        m = dict(shared)
        m["xo"] = np.ascontiguousarray(x[b, half * TOK:(half + 1) * TOK])
        hmv = np.zeros((128, 2), np.float32)
        if half == 0:
            m["xh"] = np.zeros((HALO, D), np.float32)
            hmv[:, 0] = -1e30
            hmv[:, 1] = 0.0
        else:
            m["xh"] = np.ascontiguousarray(x[b, TOK - HALO:TOK])
            hmv[:, 0] = 0.0
            hmv[:, 1] = 1.0
        m["hm"] = hmv
        in_maps.append(m)
    res = run_bass_kernel_spmd(nc, in_maps, core_ids=list(range(NCORES)))
    out = np.empty((4, 2 * TOK, D), np.float32)
    for core in range(NCORES):
        b, half = core // 2, core % 2
        out[b, half * TOK:(half + 1) * TOK] = res.results[core]["y"]
    return out
```
